# Optimizing a Trainium2 kernel written in Bass

```python
import math
import jax, jax.numpy as jnp
from jax import lax
import numpy as np

D_MODEL = 2048
BATCH = 2
SEQ = 4096
DEPTH = 2

MEM_LEN = 256
CONV_WIDTH = 1024
CONV_K = 31
SSM_WIDTH = 1024
SSM_GROUP = 16
SSM_GROUPS = SSM_WIDTH // SSM_GROUP
SSM_STATE = 64
XA_HEADS = 4
XA_HEAD_DIM = D_MODEL // XA_HEADS
D_FF = 5632
FFN_K = 3
EPS = 1e-6
DT_MIN = 1e-3
DT_MAX = 1e-1
IN_COLS = 2 * CONV_WIDTH + SSM_WIDTH + 2 * D_MODEL

kernel_name = "griffin_gated_conformer_s5_hybrid"


def rmsnorm(x, g):
    xf = x.astype(jnp.float32)
    y = xf * lax.rsqrt(jnp.mean(xf * xf, axis=-1, keepdims=True) + EPS)
    return (y * g.astype(jnp.float32)).astype(x.dtype)


def layernorm(x, g, b):
    xf = x.astype(jnp.float32)
    mu = jnp.mean(xf, axis=-1, keepdims=True)
    xc = xf - mu
    y = xc * lax.rsqrt(jnp.mean(xc * xc, axis=-1, keepdims=True) + EPS)
    return (y * g.astype(jnp.float32) + b.astype(jnp.float32)).astype(x.dtype)


def causal_dwconv(x, w):
    k = w.shape[0]
    return lax.conv_general_dilated(
        x, w[:, None, :].astype(x.dtype), window_strides=(1,), padding=[(k - 1, 0)],
        dimension_numbers=("NWC", "WIO", "NWC"), feature_group_count=x.shape[-1])


def conformer_conv_branch(u2, dw_w, dw_b, ln_g, ln_b, w_pw):
    a, b = jnp.split(u2, 2, axis=-1)
    h = a * jax.nn.sigmoid(b)
    h = causal_dwconv(h, dw_w) + dw_b.astype(h.dtype)
    h = layernorm(h, ln_g, ln_b)
    h = jax.nn.silu(h)
    return h @ w_pw


def _cmul_scan(e1, e2):
    a1r, a1i, b1r, b1i = e1
    a2r, a2i, b2r, b2i = e2
    ar = a2r * a1r - a2i * a1i
    ai = a2r * a1i + a2i * a1r
    br = a2r * b1r - a2i * b1i + b2r
    bi = a2r * b1i + a2i * b1r + b2i
    return ar, ai, br, bi


def s5_branch(u, a_re, a_im, log_dt, b_re, b_im, c_re, c_im, d_skip, w_glu):
    bsz, seq, _ = u.shape
    f32 = jnp.float32
    uf = u.astype(f32).reshape(bsz, seq, SSM_GROUPS, SSM_GROUP)
    ar = jnp.minimum(a_re.astype(f32), -1e-4)
    ai = a_im.astype(f32)
    dt = jnp.exp(log_dt.astype(f32))[:, None]
    mag = jnp.exp(dt * ar)
    abar_re = mag * jnp.cos(dt * ai)
    abar_im = mag * jnp.sin(dt * ai)
    den = ar * ar + ai * ai
    nr = abar_re - 1.0
    ni = abar_im
    z_re = (nr * ar + ni * ai) / den
    z_im = (ni * ar - nr * ai) / den
    br = b_re.astype(f32)
    bi = b_im.astype(f32)
    bbar_re = z_re[..., None] * br - z_im[..., None] * bi
    bbar_im = z_re[..., None] * bi + z_im[..., None] * br
    bu_re = jnp.einsum("blgh,gph->blgp", uf, bbar_re)
    bu_im = jnp.einsum("blgh,gph->blgp", uf, bbar_im)
    shape_a = (1, seq, SSM_GROUPS, SSM_STATE)
    a_t_re = jnp.broadcast_to(abar_re[None, None], shape_a)
    a_t_im = jnp.broadcast_to(abar_im[None, None], shape_a)
    _, _, xr, xi = lax.associative_scan(_cmul_scan, (a_t_re, a_t_im, bu_re, bu_im), axis=1)
    y = (jnp.einsum("blgp,ghp->blgh", xr, c_re.astype(f32))
         - jnp.einsum("blgp,ghp->blgh", xi, c_im.astype(f32)))
    y = y.reshape(bsz, seq, SSM_WIDTH) + d_skip.astype(f32) * uf.reshape(bsz, seq, SSM_WIDTH)
    y = jax.nn.gelu(y).astype(u.dtype)
    g = y @ w_glu
    ga, gb = jnp.split(g, 2, axis=-1)
    return ga * jax.nn.sigmoid(gb)


def cross_attention(h, m, w_q, w_kv, w_o):
    bsz, seq, _ = h.shape
    q = (h @ w_q).reshape(bsz, seq, XA_HEADS, XA_HEAD_DIM)
    k, v = jnp.split(m @ w_kv, 2, axis=-1)
    k = k.reshape(bsz, MEM_LEN, XA_HEADS, XA_HEAD_DIM)
    v = v.reshape(bsz, MEM_LEN, XA_HEADS, XA_HEAD_DIM)
    s = jnp.einsum("blhd,bmhd->bhlm", q, k).astype(jnp.float32) * (XA_HEAD_DIM ** -0.5)
    p = jax.nn.softmax(s, axis=-1).astype(v.dtype)
    o = jnp.einsum("bhlm,bmhd->blhd", p, v).reshape(bsz, seq, D_MODEL)
    return o @ w_o


def conv_ffn(h, w_up, dw_w, w_down):
    up = causal_dwconv(h @ w_up, dw_w)
    gate, val = jnp.split(up, 2, axis=-1)
    return (jax.nn.silu(gate) * val) @ w_down


def setup_inputs(seed: int = 0) -> dict:
    key = jax.random.key(seed)
    ks = iter(jax.random.split(key, 40))

    def nrm(shape, scale):
        return jax.random.normal(next(ks), shape, jnp.float32) * scale

    def gain(shape):
        return 1.0 + nrm(shape, 0.02)

    L = DEPTH
    n = jnp.arange(SSM_STATE, dtype=jnp.float32)
    a_re = -0.5 + nrm((L, SSM_GROUPS, SSM_STATE), 0.01)
    a_im = math.pi * n[None, None, :] + nrm((L, SSM_GROUPS, SSM_STATE), 0.01)
    log_dt = jax.random.uniform(next(ks), (L, SSM_GROUPS), jnp.float32,
                                math.log(DT_MIN), math.log(DT_MAX))
    return {
        "x": nrm((BATCH, SEQ, D_MODEL), 1.0),
        "mem": nrm((BATCH, MEM_LEN, D_MODEL), 1.0),
        "mix_norm_g": gain((L, D_MODEL)),
        "w_in": nrm((L, D_MODEL, IN_COLS), D_MODEL ** -0.5),
        "conv_dw_w": nrm((L, CONV_K, CONV_WIDTH), CONV_K ** -0.5),
        "conv_dw_b": nrm((L, CONV_WIDTH), 0.02),
        "conv_ln_g": gain((L, CONV_WIDTH)),
        "conv_ln_b": nrm((L, CONV_WIDTH), 0.02),
        "conv_w_pw": nrm((L, CONV_WIDTH, D_MODEL), CONV_WIDTH ** -0.5),
        "ssm_a_re": a_re,
        "ssm_a_im": a_im,
        "ssm_log_dt": log_dt,
        "ssm_b_re": nrm((L, SSM_GROUPS, SSM_STATE, SSM_GROUP), (2 * SSM_GROUP) ** -0.5),
        "ssm_b_im": nrm((L, SSM_GROUPS, SSM_STATE, SSM_GROUP), (2 * SSM_GROUP) ** -0.5),
        "ssm_c_re": nrm((L, SSM_GROUPS, SSM_GROUP, SSM_STATE), SSM_STATE ** -0.5),
        "ssm_c_im": nrm((L, SSM_GROUPS, SSM_GROUP, SSM_STATE), SSM_STATE ** -0.5),
        "ssm_d": nrm((L, SSM_WIDTH), 1.0),
        "ssm_w_glu": nrm((L, SSM_WIDTH, 2 * D_MODEL), SSM_WIDTH ** -0.5),
        "w_out": nrm((L, D_MODEL, D_MODEL), D_MODEL ** -0.5),
        "xa_norm_g": gain((L, D_MODEL)),
        "mem_norm_g": gain((L, D_MODEL)),
        "xa_w_q": nrm((L, D_MODEL, D_MODEL), D_MODEL ** -0.5),
        "xa_w_kv": nrm((L, D_MODEL, 2 * D_MODEL), D_MODEL ** -0.5),
        "xa_w_o": nrm((L, D_MODEL, D_MODEL), D_MODEL ** -0.5),
        "ffn_norm_g": gain((L, D_MODEL)),
        "ffn_w_up": nrm((L, D_MODEL, 2 * D_FF), D_MODEL ** -0.5),
        "ffn_dw_w": nrm((L, FFN_K, 2 * D_FF), FFN_K ** -0.5),
        "ffn_w_down": nrm((L, D_FF, D_MODEL), D_FF ** -0.5),
        "final_norm_g": gain((D_MODEL,)),
    }


def reference(x, mem, mix_norm_g, w_in, conv_dw_w, conv_dw_b, conv_ln_g, conv_ln_b, conv_w_pw,
              ssm_a_re, ssm_a_im, ssm_log_dt, ssm_b_re, ssm_b_im, ssm_c_re, ssm_c_im, ssm_d,
              ssm_w_glu, w_out, xa_norm_g, mem_norm_g, xa_w_q, xa_w_kv, xa_w_o,
              ffn_norm_g, ffn_w_up, ffn_dw_w, ffn_w_down, final_norm_g):
    split_pts = [2 * CONV_WIDTH, 2 * CONV_WIDTH + SSM_WIDTH]
    for i in range(DEPTH):
        h = rmsnorm(x, mix_norm_g[i])
        proj = h @ w_in[i]
        u_conv, u_ssm, gate_logits = jnp.split(proj, split_pts, axis=-1)
        y_a = conformer_conv_branch(u_conv, conv_dw_w[i], conv_dw_b[i], conv_ln_g[i],
                                    conv_ln_b[i], conv_w_pw[i])
        y_b = s5_branch(u_ssm, ssm_a_re[i], ssm_a_im[i], ssm_log_dt[i], ssm_b_re[i],
                        ssm_b_im[i], ssm_c_re[i], ssm_c_im[i], ssm_d[i], ssm_w_glu[i])
        g_a, g_b = jnp.split(jax.nn.sigmoid(gate_logits), 2, axis=-1)
        x = x + (g_a * y_a + g_b * y_b) @ w_out[i]
        h = rmsnorm(x, xa_norm_g[i])
        m = rmsnorm(mem, mem_norm_g[i])
        x = x + cross_attention(h, m, xa_w_q[i], xa_w_kv[i], xa_w_o[i])
        h = rmsnorm(x, ffn_norm_g[i])
        x = x + conv_ffn(h, ffn_w_up[i], ffn_dw_w[i], ffn_w_down[i])
    return rmsnorm(x, final_norm_g)
```

```python
import numpy as np
from contextlib import ExitStack
import concourse.bass as bass
import concourse.mybir as mybir
from concourse.bass_utils import run_bass_kernel_spmd

F32 = mybir.dt.float32
F32R = mybir.dt.float32r
I32 = mybir.dt.int32
AF = mybir.ActivationFunctionType
ALU = mybir.AluOpType
AX = mybir.AxisListType

D = 2048
KT = 16
T = 512
HAL = 30
XW = HAL + T
SEQ = 4096
MEM = 256
DFF = 5632
NHT = 44
EPS = 1e-6
NSLOT = 3
TWO_PI = 6.283185307179586
PI = 3.141592653589793

V_MIXG, V_XAG, V_MEMG, V_FFNG = 0, 16, 32, 48
V_CB, V_LNG, V_LNB, V_SD = 64, 72, 80, 88
V_CW = 96
V_FW = V_CW + 248
V_N = V_FW + 264


class Eng:
    def __init__(self, name, obj, sem):
        self.name, self.obj, self.sem, self.count, self.seen = name, obj, sem, 0, {}


class Prog:
    def __init__(self, nc, es):
        self.nc = nc
        self.E = {}
        for name, obj in (("pe", nc.tensor), ("act", nc.scalar), ("dve", nc.vector),
                          ("pool", nc.gpsimd), ("sp", nc.sync)):
            self.E[name] = Eng(name, obj, es.enter_context(nc.semaphore("s_" + name)))
        self.keys = {}
        self.floor = []
        self.es = es
        self.dsem = {}

    def _deps(self, r, w):
        evs = []
        for k in r:
            st = self.keys.get(k)
            if st and st["w"]:
                evs.append(st["w"])
        for k in w:
            st = self.keys.get(k)
            if st:
                if st["w"]:
                    evs.append(st["w"])
                evs.extend(st["r"].values())
        return evs

    def _wait(self, e, evs, nofloor=False):
        if not nofloor:
            evs = list(evs) + self.floor
        for sem, val, owner in evs:
            if owner == "pe" and e.name == "pe":
                continue
            if e.seen.get(id(sem), 0) < val:
                e.obj.wait_ge(sem, val)
                e.seen[id(sem)] = val

    def _record(self, ev, r, w, rname):
        for k in w:
            self.keys[k] = {"w": ev, "r": {}}
        for k in r:
            st = self.keys.setdefault(k, {"w": None, "r": {}})
            st["r"][rname] = ev

    def op(self, en, fn, r=(), w=(), nofloor=False):
        e = self.E[en]
        self._wait(e, self._deps(r, w), nofloor)
        ins = fn(e.obj)
        e.count += 1
        ins.then_inc(e.sem, 1)
        ev = (e.sem, e.count, en)
        self._record(ev, r, w, en)

    def dma(self, qn, outs_ins, semname, r=(), w=(), nofloor=False):
        e = self.E[qn]
        self._wait(e, self._deps(r, w), nofloor)
        if semname not in self.dsem:
            self.dsem[semname] = [self.es.enter_context(self.nc.semaphore("d_" + semname)), 0]
        ds = self.dsem[semname]
        for o, i in outs_ins:
            e.obj.dma_start(out=o, in_=i).then_inc(ds[0], 16)
            ds[1] += 16
        ev = (ds[0], ds[1], "dma")
        self._record(ev, r, w, "dma_" + semname)

    def barrier(self):
        self.floor = [(e.sem, e.count, n) for n, e in self.E.items() if n != "sp" and e.count > 0]
        self.floor += [(ds[0], ds[1], "dma") for nm, ds in self.dsem.items() if not nm.startswith("slot") and ds[1] > 0]

    def final_wait(self, en):
        e = self.E[en]
        evs = [(o.sem, o.count, n) for n, o in self.E.items() if n != en and o.count > 0]
        evs += [(ds[0], ds[1], "dma") for ds in self.dsem.values()]
        self._wait(e, evs)


def build_program(NQ=8, NL=2, dbg=None):
    nc = bass.Bass("TRN2", target_bir_lowering=False)
    nc.dge_precook = False
    es = ExitStack()
    P = Prog(nc, es)

    def din(name, shape, dt=F32):
        return nc.dram_tensor(name, list(shape), dt, kind="ExternalInput").ap()

    xT = din("xT", [D, SEQ])
    memT = din("memT", [D, MEM])
    vecs = din("vecs", [128, NL * V_N + 16])
    ident_d = din("ident", [128, 128])
    ones_d = din("ones", [128, 128])
    W = []
    for l in range(NL):
        W.append(dict(
            w_in=din(f"w_in{l}", [56 * 128, 2048], F32R),
            w_pw=din(f"w_pw{l}", [16 * 128, 1024], F32R),
            w_glu=din(f"w_glu{l}", [32 * 128, 1024], F32R),
            w_out=din(f"w_out{l}", [16 * 128, 2048], F32R),
            w_q=din(f"w_q{l}", [16 * 128, 2048], F32R),
            w_k=din(f"w_k{l}", [16 * 128, 2048], F32R),
            w_v=din(f"w_v{l}", [16 * 128, 2048], F32R),
            w_o=din(f"w_o{l}", [16 * 128, 2048], F32R),
            w_up=din(f"w_up{l}", [88 * 128, 2048], F32R),
            w_dn=din(f"w_dn{l}", [64 * 128, 1408], F32R),
            ssm_col=din(f"ssm_col{l}", [128, 3 * 32]),
            ssm_row=din(f"ssm_row{l}", [128, 3 * 4096]),
            bt=din(f"bt{l}", [128, 2 * 4096]),
            ct=din(f"ct{l}", [128, 32 * 256], F32R),
            lhs=nc.dram_tensor(f"sc_lhs{l}", [128, 32 * 512], F32R).ap(),
            tab=nc.dram_tensor(f"sc_tab{l}", [128, 32 * 256], F32).ap(),
            kt=nc.dram_tensor(f"sc_kt{l}", [128, 16 * 256], F32R).ap(),
            v=nc.dram_tensor(f"sc_v{l}", [128, 2 * 2048], F32R).ap(),
        ))
    outT = nc.dram_tensor("outT", [D, NQ * T], F32, kind="ExternalOutput").ap()
    dbg_out = None
    if dbg:
        dbg_out = nc.dram_tensor("dbg", [128, dbg[1]], F32, kind="ExternalOutput").ap()

    def sb(name, cols, dt=F32):
        return es.enter_context(nc.sbuf_tensor(name, [128, cols], dt))

    XE = sb("XE", KT * XW)
    HE = sb("HE", KT * XW, F32R)
    RAC = sb("RAC", 10240, F32R)
    RB = sb("RB", 8192, F32R)
    RT = sb("RT", 4608)
    SL = sb("SL", NSLOT * 2048, F32R)
    VEC = sb("VEC", NL * V_N + 16)
    IDN = sb("IDN", 128)
    ONE = sb("ONE", 128)
    XT30 = sb("XT30", NL * KT * HAL)
    XT2 = sb("XT2", NL * KT * 2)
    SST = sb("SST", NL * 64)
    SCOL = sb("SCOL", NL * 96)
    SM = sb("SM", 64)
    PS = [es.enter_context(nc.psum_tensor(f"ps{i}", [128, 512], F32)) for i in range(8)]
    psn = [0]
    pinned = set()

    def ps(pin=False):
        while True:
            i = psn[0] % 8
            psn[0] += 1
            if i not in pinned:
                break
        if pin:
            pinned.add(i)
        return i

    def v3(t, off, a, b):
        return t[:, off:off + a * b].rearrange("p (a b) -> p a b", a=a)

    xe = v3(XE, 0, KT, XW)
    he = v3(HE, 0, KT, XW)
    R = lambda ap: ap if ap.dtype == F32R else ap.bitcast(F32R)
    Fv = lambda ap: ap if ap.dtype == F32 else ap.bitcast(F32)

    def vcol(l, off, i):
        c = l * V_N + off + i
        return VEC[:, c:c + 1]

    P.dma("pool", [(VEC[:, :], vecs), (IDN[:, :], ident_d), (ONE[:, :], ones_d)], "const", w=["VEC", "IDN", "ONE"])
    P.op("dve", lambda e: e.memset(XT30[:, :], 0.0), w=["XT30"])
    P.op("dve", lambda e: e.memset(XT2[:, :], 0.0), w=["XT2"])
    P.op("dve", lambda e: e.memset(SST[:, :], 0.0), w=["SST"])

    slotn = [0]

    def wslot(wd, s, kt):
        i = slotn[0] % NSLOT
        slotn[0] += 1
        dst = SL[:, i * 2048:i * 2048 + kt * 128]
        P.dma("sp", [(dst, wd[s * 128:(s + 1) * 128, 0:kt * 128])], f"slot{i}", w=[("slot", i)], nofloor=True)
        return i, (lambda k, i=i: SL[:, i * 2048 + k * 128:i * 2048 + (k + 1) * 128])

    def mm(pskey, out_ap, pairs, r):
        def fn(pe):
            n = len(pairs)
            ins = None
            for idx, (l, rh) in enumerate(pairs):
                ins = pe.matmul(out_ap, l, rh, start=(idx == 0), stop=(idx == n - 1))
            return ins
        P.op("pe", fn, r=r, w=[pskey])

    def tt(en, out, a, b, op, r, w):
        P.op(en, lambda e: e.tensor_tensor(out=out, in0=a, in1=b, op=op), r=r, w=w)

    def act(out, in_, func, r, w, bias=None, scale=None, accum=None):
        kw = {}
        if bias is not None:
            kw["bias"] = bias
        if scale is not None:
            kw["scale"] = scale
        if accum is not None:
            kw["accum_out"] = accum
        P.op("act", lambda e: e.activation(out=out, in_=in_, func=func, **kw), r=r, w=w)

    def rmsnorm(src_main, src_halo, hw, gofs, l, dst_main, dst_halo, rkeys, wkeys, after=None):
        sq = v3(RT, 0, 2, XW)
        rs = RT[:, 2 * XW:3 * XW]
        pm = ps()
        ph = ps() if hw else None
        for k in range(KT):
            b = k % 2
            act(sq[:, b, 0:T], src_main(k), AF.Square, r=rkeys, w=[("sq", b)])
            if hw:
                act(sq[:, b, T:T + hw], src_halo(k), AF.Square, r=rkeys, w=[("sqh", b)])
            P.op("pe", lambda pe, k=k, b=b: pe.matmul(PS[pm][:, :], ONE[:, :], sq[:, b, 0:T], start=(k == 0), stop=(k == KT - 1)),
                 r=[("sq", b), "ONE"], w=[("ps", pm)])
            if hw:
                P.op("pe", lambda pe, k=k, b=b: pe.matmul(PS[ph][:, 0:hw], ONE[:, :], sq[:, b, T:T + hw], start=(k == 0), stop=(k == KT - 1)),
                     r=[("sqh", b), "ONE"], w=[("ps", ph)])
        act(rs[:, 0:T], PS[pm][:, :], AF.Sqrt, r=[("ps", pm)], w=["rs"], bias=EPS, scale=1.0 / D)
        if hw:
            act(rs[:, T:T + hw], PS[ph][:, 0:hw], AF.Sqrt, r=[("ps", ph)], w=["rsh"], bias=EPS, scale=1.0 / D)
        n = T + hw
        P.op("dve", lambda e: e.reciprocal(out=rs[:, 0:n], in_=rs[:, 0:n]), r=["rs", "rsh"], w=["rs", "rsh"])
        for k in range(KT):
            g = vcol(l, gofs, k) if l is not None else VEC[:, NL * V_N + k:NL * V_N + k + 1]
            P.op("dve", lambda e, k=k, g=g: e.scalar_tensor_tensor(out=dst_main(k), in0=src_main(k), scalar=g, in1=rs[:, 0:T],
                                                                  op0=ALU.mult, op1=ALU.mult),
                 r=rkeys + ["rs", "VEC"], w=wkeys)
            if hw:
                P.op("dve", lambda e, k=k, g=g: e.scalar_tensor_tensor(out=dst_halo(k), in0=src_halo(k), scalar=g, in1=rs[:, T:T + hw],
                                                                      op0=ALU.mult, op1=ALU.mult),
                     r=rkeys + ["rsh", "VEC"], w=wkeys)
            if after:
                after(k)

    def sincos(theta, n, s_out, c_out, tmp_f, tmp_i, keys):
        for shift, dst in ((0.0, s_out), (PI / 2, c_out)):
            P.op("dve", lambda e: e.tensor_scalar(out=tmp_f, in0=theta, scalar1=shift, scalar2=1.0 / TWO_PI, op0=ALU.add, op1=ALU.mult), r=keys, w=keys)
            P.op("dve", lambda e: e.tensor_copy(out=tmp_i, in_=tmp_f), r=keys, w=keys)
            P.op("dve", lambda e: e.tensor_copy(out=tmp_f, in_=tmp_i), r=keys, w=keys)
            P.op("dve", lambda e: e.scalar_tensor_tensor(out=tmp_f, in0=tmp_f, scalar=-TWO_PI, in1=theta, op0=ALU.mult, op1=ALU.add), r=keys, w=keys)
            P.op("dve", lambda e: e.tensor_scalar(out=tmp_f, in0=tmp_f, scalar1=shift, scalar2=PI, op0=ALU.add, op1=ALU.min), r=keys, w=keys)
            P.op("dve", lambda e: e.tensor_scalar(out=tmp_f, in0=tmp_f, scalar1=-PI, scalar2=None, op0=ALU.max), r=keys, w=keys)
            act(dst, tmp_f, AF.Sin, r=keys, w=keys)

    def setup_layer(l):
        w = W[l]
        P.barrier()
        K_ = ["setup"]
        col = v3(XE, 0, 3, 32)
        P.dma("pool", [(XE[:, 0:96], w["ssm_col"])], "misc", r=K_, w=K_)
        dt_ = XE[:, 96:128]
        mag = SCOL[:, l * 96:l * 96 + 32]
        th = XE[:, 128:160]
        sn = XE[:, 160:192]
        cs = XE[:, 192:224]
        tf = XE[:, 224:256]
        ti = XE[:, 256:288].bitcast(I32)
        act(dt_, col[:, 2, :], AF.Exp, r=K_, w=K_)
        P.op("dve", lambda e: e.tensor_scalar(out=col[:, 0, :], in0=col[:, 0, :], scalar1=-1e-4, scalar2=None, op0=ALU.min), r=K_, w=K_)
        tt("dve", th, dt_, col[:, 0, :], ALU.mult, K_, K_)
        act(mag, th, AF.Exp, r=K_, w=K_ + ["SCOL"])
        tt("dve", th, dt_, col[:, 1, :], ALU.mult, K_, K_)
        sincos(th, 32, sn, cs, tf, ti, K_)
        tabv = w["tab"].rearrange("p (j c) -> p j c", j=32)
        e128r = SCOL[:, l * 96 + 32:l * 96 + 64]
        e128i = SCOL[:, l * 96 + 64:l * 96 + 96]
        for hf in range(2):
            CT_ = v3(XE, 512, 16, 128)
            ST_ = v3(XE, 2560, 16, 128)
            TA = v3(XE, 4608, 16, 64)
            TB = v3(XE, 5632, 16, 64)
            hs = slice(16 * hf, 16 * hf + 16)
            P.op("dve", lambda e: e.tensor_copy(out=CT_[:, :, 0:1], in_=cs[:, hs].unsqueeze(2)), r=K_, w=K_)
            P.op("dve", lambda e: e.tensor_copy(out=ST_[:, :, 0:1], in_=sn[:, hs].unsqueeze(2)), r=K_, w=K_)
            n = 1
            while n < 128:
                cb = CT_[:, :, n - 1:n].to_broadcast([128, 16, n])
                sbb = ST_[:, :, n - 1:n].to_broadcast([128, 16, n])
                tt("dve", TA[:, :, 0:n], CT_[:, :, 0:n], cb, ALU.mult, K_, K_)
                tt("dve", TB[:, :, 0:n], ST_[:, :, 0:n], sbb, ALU.mult, K_, K_)
                tt("dve", CT_[:, :, n:2 * n], TA[:, :, 0:n], TB[:, :, 0:n], ALU.subtract, K_, K_)
                tt("dve", TA[:, :, 0:n], CT_[:, :, 0:n], sbb, ALU.mult, K_, K_)
                tt("dve", TB[:, :, 0:n], ST_[:, :, 0:n], cb, ALU.mult, K_, K_)
                tt("dve", ST_[:, :, n:2 * n], TA[:, :, 0:n], TB[:, :, 0:n], ALU.add, K_, K_)
                n *= 2
            P.op("dve", lambda e: e.tensor_copy(out=e128r[:, hs].unsqueeze(2), in_=CT_[:, :, 127:128]), r=K_, w=K_ + ["SCOL"])
            P.op("dve", lambda e: e.tensor_copy(out=e128i[:, hs].unsqueeze(2), in_=ST_[:, :, 127:128]), r=K_, w=K_ + ["SCOL"])
            P.dma("pool", [(tabv[:, hs, 0:128], CT_), (tabv[:, hs, 128:256], ST_)], "misc", r=K_, w=K_ + [("tab", l)])
        P.barrier()
        lhsv = w["lhs"].rearrange("p (j c) -> p j c", j=32)
        ctv = w["ct"].rearrange("p (j c) -> p j c", j=32)
        P.dma("pool", [(lhsv[:, :, 256:512], ctv)], "misc", r=K_, w=K_ + [("lhs", l, "c")])
        PW = 256
        for pc in range(16):
            c0 = pc * PW
            A = lambda i: XE[:, i * PW:(i + 1) * PW]
            P.dma("pool", [(A(0), w["ssm_row"][:, c0:c0 + PW]), (A(1), w["ssm_row"][:, 4096 + c0:4096 + c0 + PW]),
                           (A(2), w["ssm_row"][:, 8192 + c0:8192 + c0 + PW]),
                           (A(3), w["bt"][:, c0:c0 + PW]), (A(4), w["bt"][:, 4096 + c0:4096 + c0 + PW])], "misc", r=K_, w=K_)
            are, aim, ldt, btr, bti = A(0), A(1), A(2), A(3), A(4)
            dtv, thv, mg, sn2, cs2, tf2, den, zr, zi, q1, q2 = (A(i) for i in range(5, 16))
            ti2 = A(16).bitcast(I32)
            act(dtv, ldt, AF.Exp, r=K_, w=K_)
            P.op("dve", lambda e: e.tensor_scalar(out=are, in0=are, scalar1=-1e-4, scalar2=None, op0=ALU.min), r=K_, w=K_)
            tt("dve", thv, dtv, are, ALU.mult, K_, K_)
            act(mg, thv, AF.Exp, r=K_, w=K_)
            tt("dve", thv, dtv, aim, ALU.mult, K_, K_)
            sincos(thv, PW, sn2, cs2, tf2, ti2, K_)
            tt("dve", cs2, cs2, mg, ALU.mult, K_, K_)
            tt("dve", sn2, sn2, mg, ALU.mult, K_, K_)
            P.op("dve", lambda e: e.tensor_scalar(out=cs2, in0=cs2, scalar1=-1.0, scalar2=None, op0=ALU.add), r=K_, w=K_)
            tt("dve", den, are, are, ALU.mult, K_, K_)
            tt("dve", q1, aim, aim, ALU.mult, K_, K_)
            tt("dve", den, den, q1, ALU.add, K_, K_)
            P.op("dve", lambda e: e.reciprocal(out=den, in_=den), r=K_, w=K_)
            tt("dve", q1, cs2, are, ALU.mult, K_, K_)
            tt("dve", q2, sn2, aim, ALU.mult, K_, K_)
            tt("dve", zr, q1, q2, ALU.add, K_, K_)
            tt("dve", zr, zr, den, ALU.mult, K_, K_)
            tt("dve", q1, sn2, are, ALU.mult, K_, K_)
            tt("dve", q2, cs2, aim, ALU.mult, K_, K_)
            tt("dve", zi, q1, q2, ALU.subtract, K_, K_)
            tt("dve", zi, zi, den, ALU.mult, K_, K_)
            obr = RB[:, 0:PW]
            obi = RB[:, PW:2 * PW]
            tt("dve", q1, zr, btr, ALU.mult, K_, K_)
            tt("dve", q2, zi, bti, ALU.mult, K_, K_)
            tt("dve", obr, q1, q2, ALU.subtract, K_, K_)
            tt("dve", q1, zr, bti, ALU.mult, K_, K_)
            tt("dve", q2, zi, btr, ALU.mult, K_, K_)
            tt("dve", obi, q1, q2, ALU.add, K_, K_)
            P.dma("pool", [(lhsv[:, 2 * pc:2 * pc + 2, 0:128], obr.rearrange("p (j c) -> p j c", j=2)),
                           (lhsv[:, 2 * pc:2 * pc + 2, 128:256], obi.rearrange("p (j c) -> p j c", j=2))],
                  "misc", r=K_, w=K_ + [("lhs", l, "b")])
        P.barrier()
        mn = v3(XE, 0, KT, MEM)
        P.dma("pool", [(mn, memT.rearrange("(k p) m -> p k m", p=128))], "misc", r=K_, w=K_)
        sq = v3(RT, 0, 2, XW)
        rs = RT[:, 2 * XW:2 * XW + MEM]
        pm = ps()
        for k in range(KT):
            b = k % 2
            act(sq[:, b, 0:MEM], mn[:, k, :], AF.Square, r=K_, w=[("sq", b)])
            P.op("pe", lambda pe, k=k, b=b: pe.matmul(PS[pm][:, 0:MEM], ONE[:, :], sq[:, b, 0:MEM], start=(k == 0), stop=(k == KT - 1)),
                 r=[("sq", b), "ONE"], w=[("ps", pm)])
        act(rs, PS[pm][:, 0:MEM], AF.Sqrt, r=[("ps", pm)], w=["rs"], bias=EPS, scale=1.0 / D)
        P.op("dve", lambda e: e.reciprocal(out=rs, in_=rs), r=["rs"], w=["rs"])
        mnr = v3(RAC, 0, KT, MEM)
        for k in range(KT):
            P.op("dve", lambda e, k=k: e.scalar_tensor_tensor(out=R(mnr[:, k, :]), in0=mn[:, k, :], scalar=vcol(l, V_MEMG, k), in1=rs,
                                                             op0=ALU.mult, op1=ALU.mult), r=K_ + ["rs", "VEC"], w=["mnr"])
        ktv = v3(RB, 0, KT, MEM)
        for f in range(KT):
            si, sl = wslot(w["w_k"], f, KT)
            pk = ps()
            mm(("ps", pk), PS[pk][:, 0:MEM], [(sl(k), R(mnr[:, k, :])) for k in range(KT)], r=[("slot", si), "mnr"])
            act(R(ktv[:, f, :]), PS[pk][:, 0:MEM], AF.Copy, r=[("ps", pk)], w=["ktv"])
        P.dma("pool", [(w["kt"], R(RB[:, 0:4096]))], "misc", r=K_ + ["ktv"], w=K_ + [("kt", l)])
        vv = v3(RB, 4096, 2, 2048)
        for s in range(KT):
            si, sl = wslot(w["w_v"], s, KT)
            for mt in range(2):
                pv = ps()
                mm(("ps", pv), PS[pv][:, 0:128], [(R(mnr[:, k, mt * 128:(mt + 1) * 128]), sl(k)) for k in range(KT)],
                   r=[("slot", si), "mnr"])
                act(R(vv[:, mt, s * 128:(s + 1) * 128]), PS[pv][:, 0:128], AF.Copy, r=[("ps", pv)], w=["vv"])
        P.dma("pool", [(w["v"], R(RB[:, 4096:8192]))], "misc", r=K_ + ["vv"], w=K_ + [("v", l)])
        P.barrier()

    for l in range(NL):
        setup_layer(l)

    def chunk_layer(q, l):
        w = W[l]
        xm = lambda k: xe[:, k, HAL:XW]
        xh = lambda k: xe[:, k, 0:HAL]
        hm = lambda k: R(he[:, k, HAL:XW])
        P.barrier()
        xt30 = v3(XT30, l * KT * HAL, KT, HAL)
        P.op("pool", lambda e: e.tensor_copy(out=xe[:, :, 0:HAL], in_=xt30), r=["XT30", "x"], w=["xhalo"])
        P.op("pool", lambda e: e.tensor_copy(out=xt30, in_=xe[:, :, T:XW]), r=["x", "xhalo"], w=["XT30"])
        rmsnorm(xm, xh, HAL, V_MIXG, l, hm, lambda k: R(he[:, k, 0:HAL]), ["x", "xhalo"], ["h"])
        hg = v3(RAC, 0, 8, XW)
        dg = v3(RAC, 8 * XW, 31, 128)
        co = v3(RB, 0, 8, T)
        sg = v3(RT, 3 * XW, 2, XW)
        sq = v3(RT, 0, 2, XW)
        ps1, ps2 = ps(True), ps(True)
        for i in range(8):
            sa, la = wslot(w["w_in"], 2 * i, KT)
            sb_, lb = wslot(w["w_in"], 2 * i + 1, KT)
            pa, pb, ph = ps(), ps(), ps()
            mm(("ps", pa), PS[pa][:, :], [(la(k), hm(k)) for k in range(KT)], r=[("slot", sa), "h"])
            mm(("ps", ph), PS[ph][:, 0:HAL], [(la(k), R(he[:, k, 0:HAL])) for k in range(KT)], r=[("slot", sa), "h"])
            mm(("ps", pb), PS[pb][:, :], [(lb(k), hm(k)) for k in range(KT)], r=[("slot", sb_), "h"])
            mm(("ps", ph), PS[ph][:, 32:32 + HAL], [(lb(k), R(he[:, k, 0:HAL])) for k in range(KT)], r=[("slot", sb_), "h"])
            b = i % 2
            act(sg[:, b, HAL:XW], PS[pb][:, :], AF.Sigmoid, r=[("ps", pb)], w=[("sg", b)])
            act(sg[:, b, 0:HAL], PS[ph][:, 32:32 + HAL], AF.Sigmoid, r=[("ps", ph)], w=[("sgh", b)])
            tt("dve", R(hg[:, i, HAL:XW]), PS[pa][:, :], sg[:, b, HAL:XW], ALU.mult, [("ps", pa), ("sg", b)], [("hg", i)])
            tt("dve", R(hg[:, i, 0:HAL]), PS[ph][:, 0:HAL], sg[:, b, 0:HAL], ALU.mult, [("ps", ph), ("sgh", b)], [("hgh", i)])
            cw = VEC[:, l * V_N + V_CW + i * 31:l * V_N + V_CW + (i + 1) * 31]
            P.op("dve", lambda e, cw=cw: e.tensor_tensor(out=R(dg), in0=IDN[:, :].unsqueeze(1).to_broadcast([128, 31, 128]),
                                                        in1=cw.unsqueeze(2).to_broadcast([128, 31, 128]), op=ALU.mult),
                 r=["IDN", "VEC"], w=["dg"])
            pc = ps()
            mm(("ps", pc), PS[pc][:, :], [(R(dg[:, k, :]), R(hg[:, i, k:k + T])) for k in range(31)], r=["dg", ("hg", i), ("hgh", i)])
            act(co[:, i, :], PS[pc][:, :], AF.Identity, r=[("ps", pc), "VEC"], w=[("co", i)], bias=vcol(l, V_CB, i))
            act(sq[:, b, 0:T], Fv(co[:, i, :]), AF.Square, r=[("co", i)], w=[("sq", b)])
            P.op("pe", lambda pe, i=i: pe.matmul(PS[ps1][:, :], ONE[:, :], Fv(co[:, i, :]), start=(i == 0), stop=(i == 7)),
                 r=[("co", i), "ONE"], w=[("ps", ps1)])
            P.op("pe", lambda pe, i=i, b=b: pe.matmul(PS[ps2][:, :], ONE[:, :], sq[:, b, 0:T], start=(i == 0), stop=(i == 7)),
                 r=[("sq", b), "ONE"], w=[("ps", ps2)])
        P.barrier()
        pinned.discard(ps1)
        pinned.discard(ps2)
        mean = RT[:, 0:T]
        rstd = RT[:, T:2 * T]
        tmp = v3(RT, 2 * T, 2, T)
        act(mean, PS[ps1][:, :], AF.Copy, r=[("ps", ps1)], w=["mean", ("sq", 0)], scale=1.0 / 1024)
        tt("dve", rstd, mean, mean, ALU.mult, ["mean"], ["rstd", ("sq", 1)])
        P.op("dve", lambda e: e.scalar_tensor_tensor(out=rstd, in0=PS[ps2][:, :], scalar=1.0 / 1024, in1=rstd, op0=ALU.mult, op1=ALU.subtract),
             r=[("ps", ps2), "rstd"], w=["rstd"])
        act(rstd, rstd, AF.Sqrt, r=["rstd"], w=["rstd"], bias=EPS)
        P.op("dve", lambda e: e.reciprocal(out=rstd, in_=rstd), r=["rstd"], w=["rstd"])
        for i in range(8):
            b = i % 2
            tt("dve", tmp[:, b, :], Fv(co[:, i, :]), mean, ALU.subtract, [("co", i), "mean"], [("tmp", b), ("sg", 0), ("sg", 1), ("sgh", 0), ("sgh", 1)])
            tt("dve", tmp[:, b, :], tmp[:, b, :], rstd, ALU.mult, [("tmp", b), "rstd"], [("tmp", b)])
            act(R(co[:, i, :]), tmp[:, b, :], AF.Silu, r=[("tmp", b), "VEC"], w=[("co", i)], bias=vcol(l, V_LNB, i), scale=vcol(l, V_LNG, i))
        P.barrier()
        u = v3(RAC, 0, 8, T)
        yb = v3(RB, 4096, 8, T)
        for jj in range(8):
            si, sl = wslot(w["w_in"], 16 + jj, KT)
            pu = ps()
            mm(("ps", pu), PS[pu][:, :], [(sl(k), hm(k)) for k in range(KT)], r=[("slot", si), "h"])
            act(R(u[:, jj, :]), PS[pu][:, :], AF.Copy, r=[("ps", pu)], w=[("u", jj)])
        tq = [RT[:, i * T:(i + 1) * T] for i in range(8)]
        t1, t2, t3, t4, vr, vi, wr, wi = tq
        xr, nxi = RAC[:, 4096:4608], RAC[:, 4608:5120]
        LH0 = 5120
        lhsb = [v3(RAC, LH0 + b * 512, 4, 128) for b in range(2)]
        tabb = [v3(RT, 4096 + b * 256, 2, 128) for b in range(2)]
        lhsd = w["lhs"].rearrange("p (j c) -> p j c", j=32)
        tabd = w["tab"].rearrange("p (j c) -> p j c", j=32)
        rcol = SCOL[:, l * 96:l * 96 + 32]
        e128r = SCOL[:, l * 96 + 32:l * 96 + 64]
        e128i = SCOL[:, l * 96 + 64:l * 96 + 96]
        sst = v3(SST, l * 64, 2, 32)
        for i in range(8):
            py = ps(True)
            for jl in range(4):
                j = 4 * i + jl
                b = j % 2
                P.dma("pool", [(RAC[:, LH0 + b * 512:LH0 + (b + 1) * 512], lhsd[:, j, :])], f"lh{b}",
                      r=[("lhs", l, "b"), ("lhs", l, "c")], w=[("lhsb", b)])
                P.dma("pool", [(RT[:, 4096 + b * 256:4096 + (b + 1) * 256], tabd[:, j, :])], f"tb{b}",
                      r=[("tab", l)], w=[("tabb", b)])
                pr, pi = ps(), ps()
                mm(("ps", pr), PS[pr][:, :], [(R(lhsb[b][:, 0, :]), R(u[:, i, :]))], r=[("lhsb", b), ("u", i)])
                mm(("ps", pi), PS[pi][:, :], [(R(lhsb[b][:, 1, :]), R(u[:, i, :]))], r=[("lhsb", b), ("u", i)])
                cb = tabb[b][:, 0:1, :].to_broadcast([128, 4, 128])
                sbb = tabb[b][:, 1:2, :].to_broadcast([128, 4, 128])
                q4 = lambda ap: ap.rearrange("p (a b) -> p a b", a=4)
                tk = [("tabb", b)]
                tt("dve", q4(t1), q4(PS[pr][:, :]), cb, ALU.mult, [("ps", pr)] + tk, ["t1"])
                tt("dve", q4(t2), q4(PS[pi][:, :]), sbb, ALU.mult, [("ps", pi)] + tk, ["t2"])
                tt("pool", vr, t1, t2, ALU.add, ["t1", "t2"], ["vr"])
                tt("dve", q4(t3), q4(PS[pi][:, :]), cb, ALU.mult, [("ps", pi)] + tk, ["t3"])
                tt("dve", q4(t4), q4(PS[pr][:, :]), sbb, ALU.mult, [("ps", pr)] + tk, ["t4"])
                tt("pool", vi, t3, t4, ALU.subtract, ["t3", "t4"], ["vi"])
                rj = rcol[:, j:j + 1]
                for kk in range(4):
                    c0, c1 = kk * 128, (kk + 1) * 128
                    P.op("dve", lambda e, c0=c0, c1=c1: e.tensor_tensor_scan(out=wr[:, c0:c1], data0=rj.to_broadcast([128, 128]), data1=vr[:, c0:c1],
                                                                             initial=sst[:, 0, j:j + 1], op0=ALU.mult, op1=ALU.add),
                         r=["vr", "SST", "SCOL"], w=["wr"])
                    P.op("dve", lambda e, c0=c0, c1=c1: e.tensor_tensor_scan(out=wi[:, c0:c1], data0=rj.to_broadcast([128, 128]), data1=vi[:, c0:c1],
                                                                             initial=sst[:, 1, j:j + 1], op0=ALU.mult, op1=ALU.add),
                         r=["vi", "SST", "SCOL"], w=["wi"])
                    wrl, wil = wr[:, c1 - 1:c1], wi[:, c1 - 1:c1]
                    P.op("dve", lambda e, wil=wil: e.tensor_scalar(out=SM[:, 0:1], in0=wil, scalar1=e128i[:, j:j + 1], scalar2=None, op0=ALU.mult),
                         r=["wi", "SCOL"], w=["sm0"])
                    P.op("dve", lambda e, wil=wil: e.tensor_scalar(out=SM[:, 1:2], in0=wil, scalar1=e128r[:, j:j + 1], scalar2=None, op0=ALU.mult),
                         r=["wi", "SCOL"], w=["sm1"])
                    P.op("dve", lambda e, wrl=wrl: e.scalar_tensor_tensor(out=sst[:, 0, j:j + 1], in0=wrl, scalar=e128r[:, j:j + 1], in1=SM[:, 0:1],
                                                                         op0=ALU.mult, op1=ALU.subtract), r=["wr", "sm0", "SCOL"], w=["SST"])
                    P.op("dve", lambda e, wrl=wrl: e.scalar_tensor_tensor(out=sst[:, 1, j:j + 1], in0=wrl, scalar=e128i[:, j:j + 1], in1=SM[:, 1:2],
                                                                         op0=ALU.mult, op1=ALU.add), r=["wr", "sm1", "SCOL"], w=["SST"])
                tt("dve", q4(t1), q4(wr), cb, ALU.mult, ["wr"] + tk, ["t1"])
                tt("dve", q4(t2), q4(wi), sbb, ALU.mult, ["wi"] + tk, ["t2"])
                tt("pool", xr, t1, t2, ALU.subtract, ["t1", "t2"], ["xr"])
                tt("dve", q4(t3), q4(wr), sbb, ALU.mult, ["wr"] + tk, ["t3"])
                tt("dve", q4(t4), q4(wi), cb, ALU.mult, ["wi"] + tk, ["t4"])
                P.op("dve", lambda e: e.scalar_tensor_tensor(out=nxi, in0=t3, scalar=-1.0, in1=t4, op0=ALU.mult, op1=ALU.subtract),
                     r=["t3", "t4"], w=["nxi"])

                def fn(pe, jl=jl, b=b):
                    pe.matmul(PS[py][:, :], lhsb[b][:, 2, :], xr, start=(jl == 0), stop=False)
                    return pe.matmul(PS[py][:, :], lhsb[b][:, 3, :], nxi, start=False, stop=(jl == 3))
                P.op("pe", fn, r=[("lhsb", b), "xr", "nxi"], w=[("ps", py)])
            pinned.discard(py)
            g1, g2 = tq[4], tq[5]
            P.op("dve", lambda e, i=i: e.scalar_tensor_tensor(out=g1, in0=Fv(u[:, i, :]), scalar=vcol(l, V_SD, i), in1=PS[py][:, :],
                                                             op0=ALU.mult, op1=ALU.add), r=[("u", i), ("ps", py), "VEC"], w=["vr"])
            tt("dve", g2, g1, g1, ALU.mult, ["vr"], ["vi"])
            P.op("dve", lambda e: e.tensor_scalar(out=g2, in0=g2, scalar1=0.044715, scalar2=1.0, op0=ALU.mult, op1=ALU.add), r=["vi"], w=["vi"])
            tt("dve", g2, g2, g1, ALU.mult, ["vi", "vr"], ["vi"])
            act(g2, g2, AF.Sigmoid, r=["vi"], w=["vi"], scale=1.5957691216057308)
            tt("dve", R(yb[:, i, :]), g1, g2, ALU.mult, ["vr", "vi"], [("yb", i)])
        P.barrier()
        mg = v3(RAC, 0, KT, T)
        mt_ = [RT[:, i * T:(i + 1) * T] for i in range(5)]
        for m in range(KT):
            sga, lga = wslot(w["w_in"], 24 + 2 * m, KT)
            sgb, lgb = wslot(w["w_in"], 25 + 2 * m, KT)
            pga, pgb = ps(), ps()
            mm(("ps", pga), PS[pga][:, :], [(lga(k), hm(k)) for k in range(KT)], r=[("slot", sga), "h"])
            mm(("ps", pgb), PS[pgb][:, :], [(lgb(k), hm(k)) for k in range(KT)], r=[("slot", sgb), "h"])
            act(mt_[0], PS[pga][:, :], AF.Sigmoid, r=[("ps", pga)], w=["m0"])
            act(mt_[1], PS[pgb][:, :], AF.Sigmoid, r=[("ps", pgb)], w=["m1"])
            spw, lpw = wslot(w["w_pw"], m, 8)
            pya = ps()
            mm(("ps", pya), PS[pya][:, :], [(lpw(k), R(co[:, k, :])) for k in range(8)], r=[("slot", spw)] + [("co", k) for k in range(8)])
            sla, lla = wslot(w["w_glu"], 2 * m, 8)
            slb, llb = wslot(w["w_glu"], 2 * m + 1, 8)
            pla, plb = ps(), ps()
            ybk = [("yb", k) for k in range(8)]
            mm(("ps", pla), PS[pla][:, :], [(lla(k), R(yb[:, k, :])) for k in range(8)], r=[("slot", sla)] + ybk)
            mm(("ps", plb), PS[plb][:, :], [(llb(k), R(yb[:, k, :])) for k in range(8)], r=[("slot", slb)] + ybk)
            act(mt_[2], PS[plb][:, :], AF.Sigmoid, r=[("ps", plb)], w=["m2"])
            tt("dve", mt_[3], PS[pya][:, :], mt_[0], ALU.mult, [("ps", pya), "m0"], ["m3"])
            tt("dve", mt_[4], PS[pla][:, :], mt_[2], ALU.mult, [("ps", pla), "m2"], ["m4"])
            tt("pool", mt_[4], mt_[4], mt_[1], ALU.mult, ["m4", "m1"], ["m4"])
            tt("pool", R(mg[:, m, :]), mt_[3], mt_[4], ALU.add, ["m3", "m4"], [("mg", m)])
        mgk = [("mg", k) for k in range(KT)]
        for m in range(KT):
            si, sl = wslot(w["w_out"], m, KT)
            po = ps()
            mm(("ps", po), PS[po][:, :], [(sl(k), R(mg[:, k, :])) for k in range(KT)], r=[("slot", si)] + mgk)
            tt("dve", xm(m), xm(m), PS[po][:, :], ALU.add, [("ps", po), "x"], ["x"])
        P.barrier()
        rmsnorm(xm, None, 0, V_XAG, l, hm, None, ["x"], ["h"])
        ktv = v3(RB, 0, KT, MEM)
        vv = v3(RB, 4096, 2, 2048)
        P.dma("pool", [(R(RB[:, 0:4096]), w["kt"]), (R(RB[:, 4096:8192]), w["v"])], "kv", r=[("kt", l), ("v", l)], w=["kv"])
        qT = v3(RAC, 0, KT, T)
        for m in range(KT):
            si, sl = wslot(w["w_q"], m, KT)
            pq = ps()
            mm(("ps", pq), PS[pq][:, :], [(sl(k), hm(k)) for k in range(KT)], r=[("slot", si), "h"])
            act(R(qT[:, m, :]), PS[pq][:, :], AF.Copy, r=[("ps", pq)], w=[("q", m)])
        P.barrier()
        scale = 512 ** -0.5
        Pb = [RT[:, b * 256:(b + 1) * 256] for b in range(2)]
        PTb = [v3(RAC, 8192 + b * 1024, 2, T) for b in range(2)]
        for hd in range(4):
            pb_ = hd % 2
            for t4_ in range(4):
                b = t4_ % 2
                psc = ps()
                mm(("ps", psc), PS[psc][:, 0:MEM], [(R(qT[:, 4 * hd + d_, t4_ * 128:(t4_ + 1) * 128]), R(ktv[:, 4 * hd + d_, :])) for d_ in range(4)],
                   r=["kv"] + [("q", 4 * hd + d_) for d_ in range(4)])
                P.op("dve", lambda e: e.tensor_reduce(out=SM[:, 8:9], in_=PS[psc][:, 0:MEM], axis=AX.X, op=ALU.max), r=[("ps", psc)], w=["mx"])
                P.op("dve", lambda e: e.tensor_scalar(out=SM[:, 9:10], in0=SM[:, 8:9], scalar1=-scale, scalar2=None, op0=ALU.mult), r=["mx"], w=["nb"])
                act(Pb[b], PS[psc][:, 0:MEM], AF.Exp, r=[("ps", psc), "nb"], w=[("P", b), "rsum"], bias=SM[:, 9:10], scale=scale, accum=SM[:, 10:11])
                P.op("dve", lambda e: e.reciprocal(out=SM[:, 11:12], in_=SM[:, 10:11]), r=["rsum"], w=["ri"])
                P.op("dve", lambda e, b=b: e.tensor_scalar(out=Pb[b], in0=Pb[b], scalar1=SM[:, 11:12], scalar2=None, op0=ALU.mult), r=[("P", b), "ri"], w=[("P", b)])
                ptp = ps()

                def fn(pe, b=b, ptp=ptp):
                    pe.transpose(PS[ptp][:, 0:128], Pb[b][:, 0:128], IDN[:, :])
                    return pe.transpose(PS[ptp][:, 128:256], Pb[b][:, 128:256], IDN[:, :])
                P.op("pe", fn, r=[("P", b), "IDN"], w=[("ps", ptp)])
                for mt in range(2):
                    act(R(PTb[pb_][:, mt, t4_ * 128:(t4_ + 1) * 128]), PS[ptp][:, mt * 128:(mt + 1) * 128], AF.Copy, r=[("ps", ptp)], w=[("PT", pb_)])
            for d_ in range(4):
                f = 4 * hd + d_
                pov = ps()
                mm(("ps", pov), PS[pov][:, :], [(R(vv[:, mt, f * 128:(f + 1) * 128]), R(PTb[pb_][:, mt, :])) for mt in range(2)], r=["kv", ("PT", pb_)])
                act(hm(f), PS[pov][:, :], AF.Copy, r=[("ps", pov)], w=[("oT", f)])
        otk = [("oT", f) for f in range(KT)]
        for m in range(KT):
            si, sl = wslot(w["w_o"], m, KT)
            po = ps()
            mm(("ps", po), PS[po][:, :], [(sl(k), hm(k)) for k in range(KT)], r=[("slot", si)] + otk)
            tt("dve", xm(m), xm(m), PS[po][:, :], ALU.add, [("ps", po), "x"], ["x"])
        P.barrier()
        xt2 = v3(XT2, l * KT * 2, KT, 2)
        rmsnorm(xm, lambda k: xt2[:, k, :], 2, V_FFNG, l, hm, lambda k: R(he[:, k, HAL - 2:HAL]), ["x", "XT2"], ["h"])
        P.op("pool", lambda e: e.tensor_copy(out=xt2, in_=xe[:, :, XW - 2:XW]), r=["x"], w=["XT2"])
        P.barrier()
        hid = v3(RAC, 0, 11, T)
        EX = T + 2
        ug = [RT[:, b * EX:(b + 1) * EX] for b in range(2)]
        uv = [RT[:, (2 + b) * EX:(3 + b) * EX] for b in range(2)]
        F0 = 4 * EX
        cg, cv, sgt = RT[:, F0:F0 + T], RT[:, F0 + T:F0 + 2 * T], RT[:, F0 + 2 * T:F0 + 3 * T]
        for blk in range(4):
            for il in range(11):
                i = 11 * blk + il
                b = i % 2
                sgs, lgs = wslot(w["w_up"], 2 * i, KT)
                svs, lvs = wslot(w["w_up"], 2 * i + 1, KT)
                pg, pv, ph = ps(), ps(), ps()
                mm(("ps", pg), PS[pg][:, :], [(lgs(k), hm(k)) for k in range(KT)], r=[("slot", sgs), "h"])
                mm(("ps", ph), PS[ph][:, 0:2], [(lgs(k), R(he[:, k, HAL - 2:HAL])) for k in range(KT)], r=[("slot", sgs), "h"])
                mm(("ps", pv), PS[pv][:, :], [(lvs(k), hm(k)) for k in range(KT)], r=[("slot", svs), "h"])
                mm(("ps", ph), PS[ph][:, 8:10], [(lvs(k), R(he[:, k, HAL - 2:HAL])) for k in range(KT)], r=[("slot", svs), "h"])
                act(ug[b][:, 2:EX], PS[pg][:, :], AF.Copy, r=[("ps", pg)], w=[("ug", b)])
                act(ug[b][:, 0:2], PS[ph][:, 0:2], AF.Copy, r=[("ps", ph)], w=[("ugh", b)])
                act(uv[b][:, 2:EX], PS[pv][:, :], AF.Copy, r=[("ps", pv)], w=[("uv", b)])
                act(uv[b][:, 0:2], PS[ph][:, 8:10], AF.Copy, r=[("ps", ph)], w=[("uvh", b)])
                for (src, dst, key, tl) in ((ug[b], cg, "cg", i), (uv[b], cv, "cv", NHT + i)):
                    fw = lambda tap, tl=tl: VEC[:, l * V_N + V_FW + tl * 3 + tap:l * V_N + V_FW + tl * 3 + tap + 1]
                    sk = [("ug", b), ("ugh", b)] if key == "cg" else [("uv", b), ("uvh", b)]
                    P.op("dve", lambda e, src=src, dst=dst, fw=fw: e.tensor_scalar(out=dst, in0=src[:, 0:T], scalar1=fw(0), scalar2=None, op0=ALU.mult),
                         r=sk + ["VEC"], w=[key])
                    P.op("dve", lambda e, src=src, dst=dst, fw=fw: e.scalar_tensor_tensor(out=dst, in0=src[:, 1:T + 1], scalar=fw(1), in1=dst,
                                                                                         op0=ALU.mult, op1=ALU.add), r=sk + ["VEC", key], w=[key])
                    P.op("dve", lambda e, src=src, dst=dst, fw=fw: e.scalar_tensor_tensor(out=dst, in0=src[:, 2:T + 2], scalar=fw(2), in1=dst,
                                                                                         op0=ALU.mult, op1=ALU.add), r=sk + ["VEC", key], w=[key])
                act(sgt, cg, AF.Silu, r=["cg"], w=["sgt"])
                tt("pool", R(hid[:, il, :]), sgt, cv, ALU.mult, ["sgt", "cv"], [("hid", il)])
            hk = [("hid", k) for k in range(11)]
            for m in range(KT):
                si, sl = wslot(w["w_dn"], blk * 16 + m, 11)
                pd = ps()
                mm(("ps", pd), PS[pd][:, :], [(sl(k), R(hid[:, k, :])) for k in range(11)], r=[("slot", si)] + hk)
                tt("dve", xm(m), xm(m), PS[pd][:, :], ALU.add, [("ps", pd), "x"], ["x"])

    xTv = xT.rearrange("(k p) t -> p k t", p=128)
    oTv = outT.rearrange("(k p) t -> p k t", p=128)
    for q in range(NQ):
        P.barrier()
        P.dma("pool", [(xe[:, :, HAL:XW], xTv[:, :, q * T:(q + 1) * T])], "xin", w=["x"])
        for l in range(NL):
            chunk_layer(q, l)
        P.barrier()
        ot = v3(RT, 3 * XW, 4, T)
        rmsnorm(lambda k: xe[:, k, HAL:XW], None, 0, None, None, lambda k: ot[:, k % 4, :], None, ["x"], ["ot"],
                after=lambda k: (P.dma("pool", [(oTv[:, k - 3:k + 1, q * T:(q + 1) * T], ot)], "xout", r=["ot"], w=["outT"])
                                 if k % 4 == 3 else None))
    if dbg:
        dbg[0](P, locals())
    P.final_wait("pool")
    es.close()
    return nc


def _slots(Wm):
    K, M = Wm.shape
    kt, mt = K // 128, M // 128
    return np.ascontiguousarray(Wm.reshape(kt, 128, mt, 128).transpose(2, 1, 0, 3)).reshape(mt * 128, kt * 128)


def _tiles(Wm, order):
    return np.concatenate([Wm[:, t * 128:(t + 1) * 128] for t in order], axis=1)


def _col(v):
    n = v.shape[0] // 128
    return np.ascontiguousarray(v.reshape(n, 128).T)


def _prep_layer(inp, l):
    f = lambda a: np.asarray(a, dtype=np.float32)
    o = {}
    w_in = f(inp["w_in"][l])
    order = []
    for i in range(8):
        order += [i, 8 + i]
    order += list(range(16, 24))
    for m in range(16):
        order += [24 + m, 40 + m]
    o[f"w_in{l}"] = _slots(_tiles(w_in, order))
    o[f"w_pw{l}"] = _slots(f(inp["conv_w_pw"][l]))
    glu = f(inp["ssm_w_glu"][l])
    order = []
    for m in range(16):
        order += [m, 16 + m]
    o[f"w_glu{l}"] = _slots(_tiles(glu, order))
    o[f"w_out{l}"] = _slots(f(inp["w_out"][l]))
    o[f"w_q{l}"] = _slots(f(inp["xa_w_q"][l]))
    kv = f(inp["xa_w_kv"][l])
    o[f"w_k{l}"] = _slots(kv[:, :D])
    o[f"w_v{l}"] = _slots(kv[:, D:])
    o[f"w_o{l}"] = _slots(f(inp["xa_w_o"][l]))
    up = f(inp["ffn_w_up"][l])
    order = []
    for i in range(NHT):
        order += [i, NHT + i]
    o[f"w_up{l}"] = _slots(_tiles(up, order))
    dn = f(inp["ffn_w_down"][l])
    o[f"w_dn{l}"] = np.concatenate([_slots(dn[b * 1408:(b + 1) * 1408, :]) for b in range(4)], axis=0)
    a_re = f(inp["ssm_a_re"][l]).reshape(4096)
    a_im = f(inp["ssm_a_im"][l]).reshape(4096)
    ldt = np.repeat(f(inp["ssm_log_dt"][l]), 64)
    o[f"ssm_col{l}"] = np.concatenate([_col(a_re), _col(a_im), _col(ldt)], axis=1)
    o[f"ssm_row{l}"] = np.ascontiguousarray(np.broadcast_to(np.concatenate([a_re, a_im, ldt])[None, :], (128, 3 * 4096)))
    B_re = f(inp["ssm_b_re"][l])
    B_im = f(inp["ssm_b_im"][l])
    C_re = f(inp["ssm_c_re"][l])
    C_im = f(inp["ssm_c_im"][l])
    bt = np.zeros((2, 128, 32, 128), np.float32)
    ct = np.zeros((128, 32, 2, 128), np.float32)
    for j in range(32):
        i = j // 4
        for gg in range(2):
            g = 2 * j + gg
            k0 = 16 * (g - 8 * i)
            bt[0, k0:k0 + 16, j, gg * 64:(gg + 1) * 64] = B_re[g].T
            bt[1, k0:k0 + 16, j, gg * 64:(gg + 1) * 64] = B_im[g].T
            ct[gg * 64:(gg + 1) * 64, j, 0, k0:k0 + 16] = C_re[g].T
            ct[gg * 64:(gg + 1) * 64, j, 1, k0:k0 + 16] = C_im[g].T
    o[f"bt{l}"] = np.concatenate([bt[0].reshape(128, 4096), bt[1].reshape(128, 4096)], axis=1)
    o[f"ct{l}"] = ct.reshape(128, 32 * 256)
    vec = np.zeros((128, V_N), np.float32)
    vec[:, V_MIXG:V_MIXG + 16] = _col(f(inp["mix_norm_g"][l]))
    vec[:, V_XAG:V_XAG + 16] = _col(f(inp["xa_norm_g"][l]))
    vec[:, V_MEMG:V_MEMG + 16] = _col(f(inp["mem_norm_g"][l]))
    vec[:, V_FFNG:V_FFNG + 16] = _col(f(inp["ffn_norm_g"][l]))
    vec[:, V_CB:V_CB + 8] = _col(f(inp["conv_dw_b"][l]))
    vec[:, V_LNG:V_LNG + 8] = _col(f(inp["conv_ln_g"][l]))
    vec[:, V_LNB:V_LNB + 8] = _col(f(inp["conv_ln_b"][l]))
    vec[:, V_SD:V_SD + 8] = _col(f(inp["ssm_d"][l]))
    cw = f(inp["conv_dw_w"][l])
    vec[:, V_CW:V_CW + 248] = cw.T.reshape(8, 128, 31).transpose(1, 0, 2).reshape(128, 248)
    fw = f(inp["ffn_dw_w"][l])
    vec[:, V_FW:V_FW + 264] = fw.T.reshape(88, 128, 3).transpose(1, 0, 2).reshape(128, 264)
    return o, vec


_CACHE = {}


def _run(inputs, NQ=8, NL=2, dbg=None, cores=2):
    key = (NQ, NL, dbg is not None)
    if key not in _CACHE:
        _CACHE[key] = build_program(NQ, NL, dbg)
    nc = _CACHE[key]
    shared = {}
    vecs = np.zeros((128, NL * V_N + 16), np.float32)
    for l in range(NL):
        o, vec = _prep_layer(inputs, l)
        shared.update(o)
        vecs[:, l * V_N:(l + 1) * V_N] = vec
    vecs[:, NL * V_N:] = _col(np.asarray(inputs["final_norm_g"], np.float32))
    shared["vecs"] = vecs
    shared["ident"] = np.eye(128, dtype=np.float32)
    shared["ones"] = np.ones((128, 128), np.float32)
    x = np.asarray(inputs["x"], np.float32)
    mem = np.asarray(inputs["mem"], np.float32)
    in_maps = []
    for b in range(cores):
        m = dict(shared)
        m["xT"] = np.ascontiguousarray(x[b].T)
        m["memT"] = np.ascontiguousarray(mem[b].T)
        in_maps.append(m)
    res = run_bass_kernel_spmd(nc, in_maps, core_ids=list(range(cores)))
    return res


def kernel(**inputs):
    res = _run(inputs)
    out = np.stack([np.ascontiguousarray(res.results[b]["outT"].T) for b in range(2)], axis=0)
    return out.astype(np.float32)
```

```python
import numpy as np
from contextlib import ExitStack
import concourse.bass as bass
import concourse.mybir as mybir
from concourse.bass_utils import run_bass_kernel_spmd

F32 = mybir.dt.float32
F32R = mybir.dt.float32r
I32 = mybir.dt.int32
AF = mybir.ActivationFunctionType
ALU = mybir.AluOpType
AX = mybir.AxisListType

D = 2048
KT = 16
T = 512
HAL = 30
XW = HAL + T
SEQ = 4096
MEM = 256
DFF = 5632
NHT = 44
EPS = 1e-6
NSLOT = 3
TWO_PI = 6.283185307179586
PI = 3.141592653589793

V_MIXG, V_XAG, V_MEMG, V_FFNG = 0, 16, 32, 48
V_CB, V_LNG, V_LNB, V_SD = 64, 72, 80, 88
V_CW = 96
V_FW = V_CW + 248
V_N = V_FW + 264


class Eng:
    def __init__(self, name, obj, sem):
        self.name, self.obj, self.sem, self.count, self.seen = name, obj, sem, 0, {}


class Prog:
    def __init__(self, nc, es):
        self.nc = nc
        self.E = {}
        for name, obj in (("pe", nc.tensor), ("act", nc.scalar), ("dve", nc.vector),
                          ("pool", nc.gpsimd), ("sp", nc.sync)):
            self.E[name] = Eng(name, obj, es.enter_context(nc.semaphore("s_" + name)))
        self.keys = {}
        self.floor = []
        self.es = es
        self.dsem = {}

    def _deps(self, r, w):
        evs = []
        for k in r:
            st = self.keys.get(k)
            if st and st["w"]:
                evs.append(st["w"])
        for k in w:
            st = self.keys.get(k)
            if st:
                if st["w"]:
                    evs.append(st["w"])
                evs.extend(st["r"].values())
        return evs

    def _wait(self, e, evs, nofloor=False):
        if not nofloor:
            evs = list(evs) + self.floor
        for sem, val, owner in evs:
            if owner == "pe" and e.name == "pe":
                continue
            if e.seen.get(id(sem), 0) < val:
                e.obj.wait_ge(sem, val)
                e.seen[id(sem)] = val

    def _record(self, ev, r, w, rname):
        for k in w:
            self.keys[k] = {"w": ev, "r": {}}
        for k in r:
            st = self.keys.setdefault(k, {"w": None, "r": {}})
            st["r"][rname] = ev

    def op(self, en, fn, r=(), w=(), nofloor=False):
        e = self.E[en]
        self._wait(e, self._deps(r, w), nofloor)
        ins = fn(e.obj)
        e.count += 1
        ins.then_inc(e.sem, 1)
        ev = (e.sem, e.count, en)
        self._record(ev, r, w, en)

    def dma(self, qn, outs_ins, semname, r=(), w=(), nofloor=False):
        e = self.E[qn]
        self._wait(e, self._deps(r, w), nofloor)
        if semname not in self.dsem:
            self.dsem[semname] = [self.es.enter_context(self.nc.semaphore("d_" + semname)), 0]
        ds = self.dsem[semname]
        for o, i in outs_ins:
            e.obj.dma_start(out=o, in_=i).then_inc(ds[0], 16)
            ds[1] += 16
        ev = (ds[0], ds[1], "dma")
        self._record(ev, r, w, "dma_" + semname)

    def barrier(self):
        self.floor = [(e.sem, e.count, n) for n, e in self.E.items() if n != "sp" and e.count > 0]
        self.floor += [(ds[0], ds[1], "dma") for nm, ds in self.dsem.items() if not nm.startswith("slot") and ds[1] > 0]

    def final_wait(self, en):
        e = self.E[en]
        evs = [(o.sem, o.count, n) for n, o in self.E.items() if n != en and o.count > 0]
        evs += [(ds[0], ds[1], "dma") for ds in self.dsem.values()]
        self._wait(e, evs)


def build_program(NQ=8, NL=2, dbg=None):
    nc = bass.Bass("TRN2", target_bir_lowering=False)
    nc.dge_precook = False
    es = ExitStack()
    P = Prog(nc, es)

    def din(name, shape, dt=F32):
        return nc.dram_tensor(name, list(shape), dt, kind="ExternalInput").ap()

    xT = din("xT", [D, SEQ])
    memT = din("memT", [D, MEM])
    vecs = din("vecs", [128, NL * V_N + 16])
    ident_d = din("ident", [128, 128])
    ones_d = din("ones", [128, 128])
    W = []
    for l in range(NL):
        W.append(dict(
            w_in=din(f"w_in{l}", [56 * 128, 2048], F32R),
            w_pw=din(f"w_pw{l}", [16 * 128, 1024], F32R),
            w_glu=din(f"w_glu{l}", [32 * 128, 1024], F32R),
            w_out=din(f"w_out{l}", [16 * 128, 2048], F32R),
            w_q=din(f"w_q{l}", [16 * 128, 2048], F32R),
            w_k=din(f"w_k{l}", [16 * 128, 2048], F32R),
            w_v=din(f"w_v{l}", [16 * 128, 2048], F32R),
            w_o=din(f"w_o{l}", [16 * 128, 2048], F32R),
            w_up=din(f"w_up{l}", [88 * 128, 2048], F32R),
            w_dn=din(f"w_dn{l}", [64 * 128, 1408], F32R),
            ssm_col=din(f"ssm_col{l}", [128, 3 * 32]),
            ssm_row=din(f"ssm_row{l}", [128, 3 * 4096]),
            bt=din(f"bt{l}", [128, 2 * 4096]),
            ct=din(f"ct{l}", [128, 32 * 256], F32R),
            lhs=nc.dram_tensor(f"sc_lhs{l}", [128, 32 * 512], F32R).ap(),
            tab=nc.dram_tensor(f"sc_tab{l}", [128, 32 * 256], F32).ap(),
            kt=nc.dram_tensor(f"sc_kt{l}", [128, 16 * 256], F32R).ap(),
            v=nc.dram_tensor(f"sc_v{l}", [128, 2 * 2048], F32R).ap(),
        ))
    outT = nc.dram_tensor("outT", [D, NQ * T], F32, kind="ExternalOutput").ap()
    dbg_out = None
    if dbg:
        dbg_out = nc.dram_tensor("dbg", [128, dbg[1]], F32, kind="ExternalOutput").ap()

    def sb(name, cols, dt=F32):
        return es.enter_context(nc.sbuf_tensor(name, [128, cols], dt))

    XE = sb("XE", KT * XW)
    HE = sb("HE", KT * XW, F32R)
    RAC = sb("RAC", 10240, F32R)
    RB = sb("RB", 8192, F32R)
    RT = sb("RT", 4608)
    SL = sb("SL", NSLOT * 2048, F32R)
    VEC = sb("VEC", NL * V_N + 16)
    IDN = sb("IDN", 128)
    ONE = sb("ONE", 128)
    HT = sb("HT", NL * 8 * HAL)
    UT = sb("UT", NL * 88 * 2)
    SST = sb("SST", NL * 64)
    SCOL = sb("SCOL", NL * 96)
    SM = sb("SM", 64)
    PS = [es.enter_context(nc.psum_tensor(f"ps{i}", [128, 512], F32)) for i in range(8)]
    psn = [0]
    pinned = set()

    def ps(pin=False):
        while True:
            i = psn[0] % 8
            psn[0] += 1
            if i not in pinned:
                break
        if pin:
            pinned.add(i)
        return i

    def v3(t, off, a, b):
        return t[:, off:off + a * b].rearrange("p (a b) -> p a b", a=a)

    xe = v3(XE, 0, KT, XW)
    he = v3(HE, 0, KT, XW)
    R = lambda ap: ap if ap.dtype == F32R else ap.bitcast(F32R)
    Fv = lambda ap: ap if ap.dtype == F32 else ap.bitcast(F32)

    def vcol(l, off, i):
        c = l * V_N + off + i
        return VEC[:, c:c + 1]

    P.dma("pool", [(VEC[:, :], vecs), (IDN[:, :], ident_d), (ONE[:, :], ones_d)], "const", w=["VEC", "IDN", "ONE"])
    P.op("dve", lambda e: e.memset(HT[:, :], 0.0), w=["HT"])
    P.op("dve", lambda e: e.memset(UT[:, :], 0.0), w=["UT"])
    P.op("dve", lambda e: e.memset(SST[:, :], 0.0), w=["SST"])

    slotn = [0]

    def wslot(wd, s, kt):
        i = slotn[0] % NSLOT
        slotn[0] += 1
        dst = SL[:, i * 2048:i * 2048 + kt * 128]
        P.dma("sp", [(dst, wd[s * 128:(s + 1) * 128, 0:kt * 128])], f"slot{i}", w=[("slot", i)], nofloor=True)
        return i, (lambda k, i=i: SL[:, i * 2048 + k * 128:i * 2048 + (k + 1) * 128])

    def mm(pskey, out_ap, pairs, r):
        def fn(pe):
            n = len(pairs)
            ins = None
            for idx, (l, rh) in enumerate(pairs):
                ins = pe.matmul(out_ap, l, rh, start=(idx == 0), stop=(idx == n - 1))
            return ins
        P.op("pe", fn, r=r, w=[pskey])

    def tt(en, out, a, b, op, r, w):
        P.op(en, lambda e: e.tensor_tensor(out=out, in0=a, in1=b, op=op), r=r, w=w)

    def act(out, in_, func, r, w, bias=None, scale=None, accum=None):
        kw = {}
        if bias is not None:
            kw["bias"] = bias
        if scale is not None:
            kw["scale"] = scale
        if accum is not None:
            kw["accum_out"] = accum
        P.op("act", lambda e: e.activation(out=out, in_=in_, func=func, **kw), r=r, w=w)

    def rmsnorm(src_main, src_halo, hw, gofs, l, dst_main, dst_halo, rkeys, wkeys, after=None):
        sq = v3(RT, 0, 2, XW)
        rs = RT[:, 2 * XW:3 * XW]
        pm = ps()
        ph = ps() if hw else None
        for k in range(KT):
            b = k % 2
            act(sq[:, b, 0:T], src_main(k), AF.Square, r=rkeys, w=[("sq", b)])
            if hw:
                act(sq[:, b, T:T + hw], src_halo(k), AF.Square, r=rkeys, w=[("sqh", b)])
            P.op("pe", lambda pe, k=k, b=b: pe.matmul(PS[pm][:, :], ONE[:, :], sq[:, b, 0:T], start=(k == 0), stop=(k == KT - 1)),
                 r=[("sq", b), "ONE"], w=[("ps", pm)])
            if hw:
                P.op("pe", lambda pe, k=k, b=b: pe.matmul(PS[ph][:, 0:hw], ONE[:, :], sq[:, b, T:T + hw], start=(k == 0), stop=(k == KT - 1)),
                     r=[("sqh", b), "ONE"], w=[("ps", ph)])
        act(rs[:, 0:T], PS[pm][:, :], AF.Sqrt, r=[("ps", pm)], w=["rs"], bias=EPS, scale=1.0 / D)
        if hw:
            act(rs[:, T:T + hw], PS[ph][:, 0:hw], AF.Sqrt, r=[("ps", ph)], w=["rsh"], bias=EPS, scale=1.0 / D)
        n = T + hw
        P.op("dve", lambda e: e.reciprocal(out=rs[:, 0:n], in_=rs[:, 0:n]), r=["rs", "rsh"], w=["rs", "rsh"])
        for k in range(KT):
            g = vcol(l, gofs, k) if l is not None else VEC[:, NL * V_N + k:NL * V_N + k + 1]
            P.op("dve", lambda e, k=k, g=g: e.scalar_tensor_tensor(out=dst_main(k), in0=src_main(k), scalar=g, in1=rs[:, 0:T],
                                                                  op0=ALU.mult, op1=ALU.mult),
                 r=rkeys + ["rs", "VEC"], w=wkeys)
            if hw:
                P.op("dve", lambda e, k=k, g=g: e.scalar_tensor_tensor(out=dst_halo(k), in0=src_halo(k), scalar=g, in1=rs[:, T:T + hw],
                                                                      op0=ALU.mult, op1=ALU.mult),
                     r=rkeys + ["rsh", "VEC"], w=wkeys)
            if after:
                after(k)

    def sincos(theta, n, s_out, c_out, tmp_f, tmp_i, keys):
        for shift, dst in ((0.0, s_out), (PI / 2, c_out)):
            P.op("dve", lambda e: e.tensor_scalar(out=tmp_f, in0=theta, scalar1=shift, scalar2=1.0 / TWO_PI, op0=ALU.add, op1=ALU.mult), r=keys, w=keys)
            P.op("dve", lambda e: e.tensor_copy(out=tmp_i, in_=tmp_f), r=keys, w=keys)
            P.op("dve", lambda e: e.tensor_copy(out=tmp_f, in_=tmp_i), r=keys, w=keys)
            P.op("dve", lambda e: e.scalar_tensor_tensor(out=tmp_f, in0=tmp_f, scalar=-TWO_PI, in1=theta, op0=ALU.mult, op1=ALU.add), r=keys, w=keys)
            P.op("dve", lambda e: e.tensor_scalar(out=tmp_f, in0=tmp_f, scalar1=shift, scalar2=PI, op0=ALU.add, op1=ALU.min), r=keys, w=keys)
            P.op("dve", lambda e: e.tensor_scalar(out=tmp_f, in0=tmp_f, scalar1=-PI, scalar2=None, op0=ALU.max), r=keys, w=keys)
            act(dst, tmp_f, AF.Sin, r=keys, w=keys)

    def setup_layer(l):
        w = W[l]
        P.barrier()
        K_ = ["setup"]
        col = v3(XE, 0, 3, 32)
        P.dma("pool", [(XE[:, 0:96], w["ssm_col"])], "misc", r=K_, w=K_)
        dt_ = XE[:, 96:128]
        mag = SCOL[:, l * 96:l * 96 + 32]
        th = XE[:, 128:160]
        sn = XE[:, 160:192]
        cs = XE[:, 192:224]
        tf = XE[:, 224:256]
        ti = XE[:, 256:288].bitcast(I32)
        act(dt_, col[:, 2, :], AF.Exp, r=K_, w=K_)
        P.op("dve", lambda e: e.tensor_scalar(out=col[:, 0, :], in0=col[:, 0, :], scalar1=-1e-4, scalar2=None, op0=ALU.min), r=K_, w=K_)
        tt("dve", th, dt_, col[:, 0, :], ALU.mult, K_, K_)
        act(mag, th, AF.Exp, r=K_, w=K_ + ["SCOL"])
        tt("dve", th, dt_, col[:, 1, :], ALU.mult, K_, K_)
        sincos(th, 32, sn, cs, tf, ti, K_)
        tabv = w["tab"].rearrange("p (j c) -> p j c", j=32)
        e128r = SCOL[:, l * 96 + 32:l * 96 + 64]
        e128i = SCOL[:, l * 96 + 64:l * 96 + 96]
        for hf in range(2):
            CT_ = v3(XE, 512, 16, 128)
            ST_ = v3(XE, 2560, 16, 128)
            TA = v3(XE, 4608, 16, 64)
            TB = v3(XE, 5632, 16, 64)
            hs = slice(16 * hf, 16 * hf + 16)
            P.op("dve", lambda e: e.tensor_copy(out=CT_[:, :, 0:1], in_=cs[:, hs].unsqueeze(2)), r=K_, w=K_)
            P.op("dve", lambda e: e.tensor_copy(out=ST_[:, :, 0:1], in_=sn[:, hs].unsqueeze(2)), r=K_, w=K_)
            n = 1
            while n < 128:
                cb = CT_[:, :, n - 1:n].to_broadcast([128, 16, n])
                sbb = ST_[:, :, n - 1:n].to_broadcast([128, 16, n])
                tt("dve", TA[:, :, 0:n], CT_[:, :, 0:n], cb, ALU.mult, K_, K_)
                tt("dve", TB[:, :, 0:n], ST_[:, :, 0:n], sbb, ALU.mult, K_, K_)
                tt("dve", CT_[:, :, n:2 * n], TA[:, :, 0:n], TB[:, :, 0:n], ALU.subtract, K_, K_)
                tt("dve", TA[:, :, 0:n], CT_[:, :, 0:n], sbb, ALU.mult, K_, K_)
                tt("dve", TB[:, :, 0:n], ST_[:, :, 0:n], cb, ALU.mult, K_, K_)
                tt("dve", ST_[:, :, n:2 * n], TA[:, :, 0:n], TB[:, :, 0:n], ALU.add, K_, K_)
                n *= 2
            P.op("dve", lambda e: e.tensor_copy(out=e128r[:, hs].unsqueeze(2), in_=CT_[:, :, 127:128]), r=K_, w=K_ + ["SCOL"])
            P.op("dve", lambda e: e.tensor_copy(out=e128i[:, hs].unsqueeze(2), in_=ST_[:, :, 127:128]), r=K_, w=K_ + ["SCOL"])
            P.dma("pool", [(tabv[:, hs, 0:128], CT_), (tabv[:, hs, 128:256], ST_)], "misc", r=K_, w=K_ + [("tab", l)])
        P.barrier()
        lhsv = w["lhs"].rearrange("p (j c) -> p j c", j=32)
        ctv = w["ct"].rearrange("p (j c) -> p j c", j=32)
        P.dma("pool", [(lhsv[:, :, 256:512], ctv)], "misc", r=K_, w=K_ + [("lhs", l, "c")])
        PW = 256
        for pc in range(16):
            c0 = pc * PW
            A = lambda i: XE[:, i * PW:(i + 1) * PW]
            P.dma("pool", [(A(0), w["ssm_row"][:, c0:c0 + PW]), (A(1), w["ssm_row"][:, 4096 + c0:4096 + c0 + PW]),
                           (A(2), w["ssm_row"][:, 8192 + c0:8192 + c0 + PW]),
                           (A(3), w["bt"][:, c0:c0 + PW]), (A(4), w["bt"][:, 4096 + c0:4096 + c0 + PW])], "misc", r=K_, w=K_)
            are, aim, ldt, btr, bti = A(0), A(1), A(2), A(3), A(4)
            dtv, thv, mg, sn2, cs2, tf2, den, zr, zi, q1, q2 = (A(i) for i in range(5, 16))
            ti2 = A(16).bitcast(I32)
            act(dtv, ldt, AF.Exp, r=K_, w=K_)
            P.op("dve", lambda e: e.tensor_scalar(out=are, in0=are, scalar1=-1e-4, scalar2=None, op0=ALU.min), r=K_, w=K_)
            tt("dve", thv, dtv, are, ALU.mult, K_, K_)
            act(mg, thv, AF.Exp, r=K_, w=K_)
            tt("dve", thv, dtv, aim, ALU.mult, K_, K_)
            sincos(thv, PW, sn2, cs2, tf2, ti2, K_)
            tt("dve", cs2, cs2, mg, ALU.mult, K_, K_)
            tt("dve", sn2, sn2, mg, ALU.mult, K_, K_)
            P.op("dve", lambda e: e.tensor_scalar(out=cs2, in0=cs2, scalar1=-1.0, scalar2=None, op0=ALU.add), r=K_, w=K_)
            tt("dve", den, are, are, ALU.mult, K_, K_)
            tt("dve", q1, aim, aim, ALU.mult, K_, K_)
            tt("dve", den, den, q1, ALU.add, K_, K_)
            P.op("dve", lambda e: e.reciprocal(out=den, in_=den), r=K_, w=K_)
            tt("dve", q1, cs2, are, ALU.mult, K_, K_)
            tt("dve", q2, sn2, aim, ALU.mult, K_, K_)
            tt("dve", zr, q1, q2, ALU.add, K_, K_)
            tt("dve", zr, zr, den, ALU.mult, K_, K_)
            tt("dve", q1, sn2, are, ALU.mult, K_, K_)
            tt("dve", q2, cs2, aim, ALU.mult, K_, K_)
            tt("dve", zi, q1, q2, ALU.subtract, K_, K_)
            tt("dve", zi, zi, den, ALU.mult, K_, K_)
            obr = RB[:, 0:PW]
            obi = RB[:, PW:2 * PW]
            tt("dve", q1, zr, btr, ALU.mult, K_, K_)
            tt("dve", q2, zi, bti, ALU.mult, K_, K_)
            tt("dve", obr, q1, q2, ALU.subtract, K_, K_)
            tt("dve", q1, zr, bti, ALU.mult, K_, K_)
            tt("dve", q2, zi, btr, ALU.mult, K_, K_)
            tt("dve", obi, q1, q2, ALU.add, K_, K_)
            P.dma("pool", [(lhsv[:, 2 * pc:2 * pc + 2, 0:128], obr.rearrange("p (j c) -> p j c", j=2)),
                           (lhsv[:, 2 * pc:2 * pc + 2, 128:256], obi.rearrange("p (j c) -> p j c", j=2))],
                  "misc", r=K_, w=K_ + [("lhs", l, "b")])
        P.barrier()
        mn = v3(XE, 0, KT, MEM)
        P.dma("pool", [(mn, memT.rearrange("(k p) m -> p k m", p=128))], "misc", r=K_, w=K_)
        sq = v3(RT, 0, 2, XW)
        rs = RT[:, 2 * XW:2 * XW + MEM]
        pm = ps()
        for k in range(KT):
            b = k % 2
            act(sq[:, b, 0:MEM], mn[:, k, :], AF.Square, r=K_, w=[("sq", b)])
            P.op("pe", lambda pe, k=k, b=b: pe.matmul(PS[pm][:, 0:MEM], ONE[:, :], sq[:, b, 0:MEM], start=(k == 0), stop=(k == KT - 1)),
                 r=[("sq", b), "ONE"], w=[("ps", pm)])
        act(rs, PS[pm][:, 0:MEM], AF.Sqrt, r=[("ps", pm)], w=["rs"], bias=EPS, scale=1.0 / D)
        P.op("dve", lambda e: e.reciprocal(out=rs, in_=rs), r=["rs"], w=["rs"])
        mnr = v3(RAC, 0, KT, MEM)
        for k in range(KT):
            P.op("dve", lambda e, k=k: e.scalar_tensor_tensor(out=R(mnr[:, k, :]), in0=mn[:, k, :], scalar=vcol(l, V_MEMG, k), in1=rs,
                                                             op0=ALU.mult, op1=ALU.mult), r=K_ + ["rs", "VEC"], w=["mnr"])
        ktv = v3(RB, 0, KT, MEM)
        for f in range(KT):
            si, sl = wslot(w["w_k"], f, KT)
            pk = ps()
            mm(("ps", pk), PS[pk][:, 0:MEM], [(sl(k), R(mnr[:, k, :])) for k in range(KT)], r=[("slot", si), "mnr"])
            act(R(ktv[:, f, :]), PS[pk][:, 0:MEM], AF.Copy, r=[("ps", pk)], w=["ktv"])
        P.dma("pool", [(w["kt"], R(RB[:, 0:4096]))], "misc", r=K_ + ["ktv"], w=K_ + [("kt", l)])
        vv = v3(RB, 4096, 2, 2048)
        for s in range(KT):
            si, sl = wslot(w["w_v"], s, KT)
            for mt in range(2):
                pv = ps()
                mm(("ps", pv), PS[pv][:, 0:128], [(R(mnr[:, k, mt * 128:(mt + 1) * 128]), sl(k)) for k in range(KT)],
                   r=[("slot", si), "mnr"])
                act(R(vv[:, mt, s * 128:(s + 1) * 128]), PS[pv][:, 0:128], AF.Copy, r=[("ps", pv)], w=["vv"])
        P.dma("pool", [(w["v"], R(RB[:, 4096:8192]))], "misc", r=K_ + ["vv"], w=K_ + [("v", l)])
        P.barrier()

    for l in range(NL):
        setup_layer(l)

    def chunk_layer(q, l):
        w = W[l]
        xm = lambda k: xe[:, k, HAL:XW]
        xh = lambda k: xe[:, k, 0:HAL]
        hm = lambda k: R(he[:, k, HAL:XW])
        P.barrier()
        rmsnorm(xm, None, 0, V_MIXG, l, hm, None, ["x"], ["h"])
        ht = v3(HT, l * 8 * HAL, 8, HAL)
        hg = v3(RAC, 0, 8, XW)
        dg = v3(RAC, 8 * XW, 31, 128)
        co = v3(RB, 0, 8, T)
        sg = v3(RT, 3 * XW, 2, XW)
        sq = v3(RT, 0, 2, XW)
        ps1, ps2 = ps(True), ps(True)
        for i in range(8):
            sa, la = wslot(w["w_in"], 2 * i, KT)
            sb_, lb = wslot(w["w_in"], 2 * i + 1, KT)
            pa, pb = ps(), ps()
            mm(("ps", pa), PS[pa][:, :], [(la(k), hm(k)) for k in range(KT)], r=[("slot", sa), "h"])
            mm(("ps", pb), PS[pb][:, :], [(lb(k), hm(k)) for k in range(KT)], r=[("slot", sb_), "h"])
            b = i % 2
            act(sg[:, b, HAL:XW], PS[pb][:, :], AF.Sigmoid, r=[("ps", pb)], w=[("sg", b)])
            P.op("pool", lambda e, i=i: e.tensor_copy(out=hg[:, i, 0:HAL], in_=ht[:, i, :]), r=[("ht", i), "HT"], w=[("hgh", i)])
            tt("dve", R(hg[:, i, HAL:XW]), PS[pa][:, :], sg[:, b, HAL:XW], ALU.mult, [("ps", pa), ("sg", b)], [("hg", i)])
            P.op("pool", lambda e, i=i: e.tensor_copy(out=ht[:, i, :], in_=Fv(hg[:, i, T:XW])), r=[("hg", i), ("hgh", i)], w=[("ht", i)])
            cw = VEC[:, l * V_N + V_CW + i * 31:l * V_N + V_CW + (i + 1) * 31]
            P.op("dve", lambda e, cw=cw: e.tensor_tensor(out=R(dg), in0=IDN[:, :].unsqueeze(1).to_broadcast([128, 31, 128]),
                                                        in1=cw.unsqueeze(2).to_broadcast([128, 31, 128]), op=ALU.mult),
                 r=["IDN", "VEC"], w=["dg"])
            pc = ps()
            mm(("ps", pc), PS[pc][:, :], [(R(dg[:, k, :]), R(hg[:, i, k:k + T])) for k in range(31)], r=["dg", ("hg", i), ("hgh", i)])
            act(co[:, i, :], PS[pc][:, :], AF.Identity, r=[("ps", pc), "VEC"], w=[("co", i)], bias=vcol(l, V_CB, i))
            act(sq[:, b, 0:T], Fv(co[:, i, :]), AF.Square, r=[("co", i)], w=[("sq", b)])
            P.op("pe", lambda pe, i=i: pe.matmul(PS[ps1][:, :], ONE[:, :], Fv(co[:, i, :]), start=(i == 0), stop=(i == 7)),
                 r=[("co", i), "ONE"], w=[("ps", ps1)])
            P.op("pe", lambda pe, i=i, b=b: pe.matmul(PS[ps2][:, :], ONE[:, :], sq[:, b, 0:T], start=(i == 0), stop=(i == 7)),
                 r=[("sq", b), "ONE"], w=[("ps", ps2)])
        P.barrier()
        pinned.discard(ps1)
        pinned.discard(ps2)
        mean = RT[:, 0:T]
        rstd = RT[:, T:2 * T]
        tmp = v3(RT, 2 * T, 2, T)
        act(mean, PS[ps1][:, :], AF.Copy, r=[("ps", ps1)], w=["mean", ("sq", 0)], scale=1.0 / 1024)
        tt("dve", rstd, mean, mean, ALU.mult, ["mean"], ["rstd", ("sq", 1)])
        P.op("dve", lambda e: e.scalar_tensor_tensor(out=rstd, in0=PS[ps2][:, :], scalar=1.0 / 1024, in1=rstd, op0=ALU.mult, op1=ALU.subtract),
             r=[("ps", ps2), "rstd"], w=["rstd"])
        act(rstd, rstd, AF.Sqrt, r=["rstd"], w=["rstd"], bias=EPS)
        P.op("dve", lambda e: e.reciprocal(out=rstd, in_=rstd), r=["rstd"], w=["rstd"])
        for i in range(8):
            b = i % 2
            tt("dve", tmp[:, b, :], Fv(co[:, i, :]), mean, ALU.subtract, [("co", i), "mean"], [("tmp", b), ("sg", 0), ("sg", 1), ("sgh", 0), ("sgh", 1)])
            tt("dve", tmp[:, b, :], tmp[:, b, :], rstd, ALU.mult, [("tmp", b), "rstd"], [("tmp", b)])
            act(R(co[:, i, :]), tmp[:, b, :], AF.Silu, r=[("tmp", b), "VEC"], w=[("co", i)], bias=vcol(l, V_LNB, i), scale=vcol(l, V_LNG, i))
        P.barrier()
        u = v3(RAC, 0, 8, T)
        yb = v3(RB, 4096, 8, T)
        for jj in range(8):
            si, sl = wslot(w["w_in"], 16 + jj, KT)
            pu = ps()
            mm(("ps", pu), PS[pu][:, :], [(sl(k), hm(k)) for k in range(KT)], r=[("slot", si), "h"])
            act(R(u[:, jj, :]), PS[pu][:, :], AF.Copy, r=[("ps", pu)], w=[("u", jj)])
        tq = [RT[:, i * T:(i + 1) * T] for i in range(8)]
        t1, t2, t3, t4, vr, vi, wr, wi = tq
        xr, nxi = RAC[:, 4096:4608], RAC[:, 4608:5120]
        LH0 = 5120
        lhsb = [v3(RAC, LH0 + b * 512, 4, 128) for b in range(2)]
        tabb = [v3(RT, 4096 + b * 256, 2, 128) for b in range(2)]
        lhsd = w["lhs"].rearrange("p (j c) -> p j c", j=32)
        tabd = w["tab"].rearrange("p (j c) -> p j c", j=32)
        rcol = SCOL[:, l * 96:l * 96 + 32]
        e128r = SCOL[:, l * 96 + 32:l * 96 + 64]
        e128i = SCOL[:, l * 96 + 64:l * 96 + 96]
        sst = v3(SST, l * 64, 2, 32)
        for i in range(8):
            py = ps(True)
            for jl in range(4):
                j = 4 * i + jl
                b = j % 2
                P.dma("pool", [(RAC[:, LH0 + b * 512:LH0 + (b + 1) * 512], lhsd[:, j, :])], f"lh{b}",
                      r=[("lhs", l, "b"), ("lhs", l, "c")], w=[("lhsb", b)])
                P.dma("pool", [(RT[:, 4096 + b * 256:4096 + (b + 1) * 256], tabd[:, j, :])], f"tb{b}",
                      r=[("tab", l)], w=[("tabb", b)])
                pr, pi = ps(), ps()
                mm(("ps", pr), PS[pr][:, :], [(R(lhsb[b][:, 0, :]), R(u[:, i, :]))], r=[("lhsb", b), ("u", i)])
                mm(("ps", pi), PS[pi][:, :], [(R(lhsb[b][:, 1, :]), R(u[:, i, :]))], r=[("lhsb", b), ("u", i)])
                cb = tabb[b][:, 0:1, :].to_broadcast([128, 4, 128])
                sbb = tabb[b][:, 1:2, :].to_broadcast([128, 4, 128])
                q4 = lambda ap: ap.rearrange("p (a b) -> p a b", a=4)
                tk = [("tabb", b)]
                tt("dve", q4(t1), q4(PS[pr][:, :]), cb, ALU.mult, [("ps", pr)] + tk, ["t1"])
                tt("dve", q4(t2), q4(PS[pi][:, :]), sbb, ALU.mult, [("ps", pi)] + tk, ["t2"])
                tt("pool", vr, t1, t2, ALU.add, ["t1", "t2"], ["vr"])
                tt("dve", q4(t3), q4(PS[pi][:, :]), cb, ALU.mult, [("ps", pi)] + tk, ["t3"])
                tt("dve", q4(t4), q4(PS[pr][:, :]), sbb, ALU.mult, [("ps", pr)] + tk, ["t4"])
                tt("pool", vi, t3, t4, ALU.subtract, ["t3", "t4"], ["vi"])
                rj = rcol[:, j:j + 1]
                for kk in range(4):
                    c0, c1 = kk * 128, (kk + 1) * 128
                    P.op("dve", lambda e, c0=c0, c1=c1: e.tensor_tensor_scan(out=wr[:, c0:c1], data0=rj.to_broadcast([128, 128]), data1=vr[:, c0:c1],
                                                                             initial=sst[:, 0, j:j + 1], op0=ALU.mult, op1=ALU.add),
                         r=["vr", "SST", "SCOL"], w=["wr"])
                    P.op("dve", lambda e, c0=c0, c1=c1: e.tensor_tensor_scan(out=wi[:, c0:c1], data0=rj.to_broadcast([128, 128]), data1=vi[:, c0:c1],
                                                                             initial=sst[:, 1, j:j + 1], op0=ALU.mult, op1=ALU.add),
                         r=["vi", "SST", "SCOL"], w=["wi"])
                    wrl, wil = wr[:, c1 - 1:c1], wi[:, c1 - 1:c1]
                    P.op("dve", lambda e, wil=wil: e.tensor_scalar(out=SM[:, 0:1], in0=wil, scalar1=e128i[:, j:j + 1], scalar2=None, op0=ALU.mult),
                         r=["wi", "SCOL"], w=["sm0"])
                    P.op("dve", lambda e, wil=wil: e.tensor_scalar(out=SM[:, 1:2], in0=wil, scalar1=e128r[:, j:j + 1], scalar2=None, op0=ALU.mult),
                         r=["wi", "SCOL"], w=["sm1"])
                    P.op("dve", lambda e, wrl=wrl: e.scalar_tensor_tensor(out=sst[:, 0, j:j + 1], in0=wrl, scalar=e128r[:, j:j + 1], in1=SM[:, 0:1],
                                                                         op0=ALU.mult, op1=ALU.subtract), r=["wr", "sm0", "SCOL"], w=["SST"])
                    P.op("dve", lambda e, wrl=wrl: e.scalar_tensor_tensor(out=sst[:, 1, j:j + 1], in0=wrl, scalar=e128i[:, j:j + 1], in1=SM[:, 1:2],
                                                                         op0=ALU.mult, op1=ALU.add), r=["wr", "sm1", "SCOL"], w=["SST"])
                tt("dve", q4(t1), q4(wr), cb, ALU.mult, ["wr"] + tk, ["t1"])
                tt("dve", q4(t2), q4(wi), sbb, ALU.mult, ["wi"] + tk, ["t2"])
                tt("pool", xr, t1, t2, ALU.subtract, ["t1", "t2"], ["xr"])
                tt("dve", q4(t3), q4(wr), sbb, ALU.mult, ["wr"] + tk, ["t3"])
                tt("dve", q4(t4), q4(wi), cb, ALU.mult, ["wi"] + tk, ["t4"])
                P.op("dve", lambda e: e.scalar_tensor_tensor(out=nxi, in0=t3, scalar=-1.0, in1=t4, op0=ALU.mult, op1=ALU.subtract),
                     r=["t3", "t4"], w=["nxi"])

                def fn(pe, jl=jl, b=b):
                    pe.matmul(PS[py][:, :], lhsb[b][:, 2, :], xr, start=(jl == 0), stop=False)
                    return pe.matmul(PS[py][:, :], lhsb[b][:, 3, :], nxi, start=False, stop=(jl == 3))
                P.op("pe", fn, r=[("lhsb", b), "xr", "nxi"], w=[("ps", py)])
            pinned.discard(py)
            g1, g2 = tq[4], tq[5]
            P.op("dve", lambda e, i=i: e.scalar_tensor_tensor(out=g1, in0=Fv(u[:, i, :]), scalar=vcol(l, V_SD, i), in1=PS[py][:, :],
                                                             op0=ALU.mult, op1=ALU.add), r=[("u", i), ("ps", py), "VEC"], w=["vr"])
            tt("dve", g2, g1, g1, ALU.mult, ["vr"], ["vi"])
            P.op("dve", lambda e: e.tensor_scalar(out=g2, in0=g2, scalar1=0.044715, scalar2=1.0, op0=ALU.mult, op1=ALU.add), r=["vi"], w=["vi"])
            tt("dve", g2, g2, g1, ALU.mult, ["vi", "vr"], ["vi"])
            act(g2, g2, AF.Sigmoid, r=["vi"], w=["vi"], scale=1.5957691216057308)
            tt("dve", R(yb[:, i, :]), g1, g2, ALU.mult, ["vr", "vi"], [("yb", i)])
        P.barrier()
        mg = v3(RAC, 0, KT, T)
        mt_ = [RT[:, i * T:(i + 1) * T] for i in range(5)]
        for m in range(KT):
            sga, lga = wslot(w["w_in"], 24 + 2 * m, KT)
            sgb, lgb = wslot(w["w_in"], 25 + 2 * m, KT)
            pga, pgb = ps(), ps()
            mm(("ps", pga), PS[pga][:, :], [(lga(k), hm(k)) for k in range(KT)], r=[("slot", sga), "h"])
            mm(("ps", pgb), PS[pgb][:, :], [(lgb(k), hm(k)) for k in range(KT)], r=[("slot", sgb), "h"])
            act(mt_[0], PS[pga][:, :], AF.Sigmoid, r=[("ps", pga)], w=["m0"])
            act(mt_[1], PS[pgb][:, :], AF.Sigmoid, r=[("ps", pgb)], w=["m1"])
            spw, lpw = wslot(w["w_pw"], m, 8)
            pya = ps()
            mm(("ps", pya), PS[pya][:, :], [(lpw(k), R(co[:, k, :])) for k in range(8)], r=[("slot", spw)] + [("co", k) for k in range(8)])
            sla, lla = wslot(w["w_glu"], 2 * m, 8)
            slb, llb = wslot(w["w_glu"], 2 * m + 1, 8)
            pla, plb = ps(), ps()
            ybk = [("yb", k) for k in range(8)]
            mm(("ps", pla), PS[pla][:, :], [(lla(k), R(yb[:, k, :])) for k in range(8)], r=[("slot", sla)] + ybk)
            mm(("ps", plb), PS[plb][:, :], [(llb(k), R(yb[:, k, :])) for k in range(8)], r=[("slot", slb)] + ybk)
            act(mt_[2], PS[plb][:, :], AF.Sigmoid, r=[("ps", plb)], w=["m2"])
            tt("dve", mt_[3], PS[pya][:, :], mt_[0], ALU.mult, [("ps", pya), "m0"], ["m3"])
            tt("dve", mt_[4], PS[pla][:, :], mt_[2], ALU.mult, [("ps", pla), "m2"], ["m4"])
            tt("pool", mt_[4], mt_[4], mt_[1], ALU.mult, ["m4", "m1"], ["m4"])
            tt("pool", R(mg[:, m, :]), mt_[3], mt_[4], ALU.add, ["m3", "m4"], [("mg", m)])
        mgk = [("mg", k) for k in range(KT)]
        for m in range(KT):
            si, sl = wslot(w["w_out"], m, KT)
            po = ps()
            mm(("ps", po), PS[po][:, :], [(sl(k), R(mg[:, k, :])) for k in range(KT)], r=[("slot", si)] + mgk)
            tt("dve", xm(m), xm(m), PS[po][:, :], ALU.add, [("ps", po), "x"], ["x"])
        P.barrier()
        rmsnorm(xm, None, 0, V_XAG, l, hm, None, ["x"], ["h"])
        ktv = v3(RB, 0, KT, MEM)
        vv = v3(RB, 4096, 2, 2048)
        P.dma("pool", [(R(RB[:, 0:4096]), w["kt"]), (R(RB[:, 4096:8192]), w["v"])], "kv", r=[("kt", l), ("v", l)], w=["kv"])
        qT = v3(RAC, 0, KT, T)
        for m in range(KT):
            si, sl = wslot(w["w_q"], m, KT)
            pq = ps()
            mm(("ps", pq), PS[pq][:, :], [(sl(k), hm(k)) for k in range(KT)], r=[("slot", si), "h"])
            act(R(qT[:, m, :]), PS[pq][:, :], AF.Copy, r=[("ps", pq)], w=[("q", m)])
        P.barrier()
        scale = 512 ** -0.5
        Pb = [RT[:, b * 256:(b + 1) * 256] for b in range(2)]
        PTb = [v3(RAC, 8192 + b * 1024, 2, T) for b in range(2)]
        for hd in range(4):
            pb_ = hd % 2
            for t4_ in range(4):
                b = t4_ % 2
                psc = ps()
                mm(("ps", psc), PS[psc][:, 0:MEM], [(R(qT[:, 4 * hd + d_, t4_ * 128:(t4_ + 1) * 128]), R(ktv[:, 4 * hd + d_, :])) for d_ in range(4)],
                   r=["kv"] + [("q", 4 * hd + d_) for d_ in range(4)])
                P.op("dve", lambda e: e.tensor_reduce(out=SM[:, 8:9], in_=PS[psc][:, 0:MEM], axis=AX.X, op=ALU.max), r=[("ps", psc)], w=["mx"])
                P.op("dve", lambda e: e.tensor_scalar(out=SM[:, 9:10], in0=SM[:, 8:9], scalar1=-scale, scalar2=None, op0=ALU.mult), r=["mx"], w=["nb"])
                act(Pb[b], PS[psc][:, 0:MEM], AF.Exp, r=[("ps", psc), "nb"], w=[("P", b), "rsum"], bias=SM[:, 9:10], scale=scale, accum=SM[:, 10:11])
                P.op("dve", lambda e: e.reciprocal(out=SM[:, 11:12], in_=SM[:, 10:11]), r=["rsum"], w=["ri"])
                P.op("dve", lambda e, b=b: e.tensor_scalar(out=Pb[b], in0=Pb[b], scalar1=SM[:, 11:12], scalar2=None, op0=ALU.mult), r=[("P", b), "ri"], w=[("P", b)])
                ptp = ps()

                def fn(pe, b=b, ptp=ptp):
                    pe.transpose(PS[ptp][:, 0:128], Pb[b][:, 0:128], IDN[:, :])
                    return pe.transpose(PS[ptp][:, 128:256], Pb[b][:, 128:256], IDN[:, :])
                P.op("pe", fn, r=[("P", b), "IDN"], w=[("ps", ptp)])
                for mt in range(2):
                    act(R(PTb[pb_][:, mt, t4_ * 128:(t4_ + 1) * 128]), PS[ptp][:, mt * 128:(mt + 1) * 128], AF.Copy, r=[("ps", ptp)], w=[("PT", pb_)])
            for d_ in range(4):
                f = 4 * hd + d_
                pov = ps()
                mm(("ps", pov), PS[pov][:, :], [(R(vv[:, mt, f * 128:(f + 1) * 128]), R(PTb[pb_][:, mt, :])) for mt in range(2)], r=["kv", ("PT", pb_)])
                act(hm(f), PS[pov][:, :], AF.Copy, r=[("ps", pov)], w=[("oT", f)])
        otk = [("oT", f) for f in range(KT)]
        for m in range(KT):
            si, sl = wslot(w["w_o"], m, KT)
            po = ps()
            mm(("ps", po), PS[po][:, :], [(sl(k), hm(k)) for k in range(KT)], r=[("slot", si)] + otk)
            tt("dve", xm(m), xm(m), PS[po][:, :], ALU.add, [("ps", po), "x"], ["x"])
        P.barrier()
        rmsnorm(xm, None, 0, V_FFNG, l, hm, None, ["x"], ["h"])
        ut = v3(UT, l * 88 * 2, 88, 2)
        P.barrier()
        hid = v3(RAC, 0, 11, T)
        EX = T + 2
        ug = [RT[:, b * EX:(b + 1) * EX] for b in range(2)]
        uv = [RT[:, (2 + b) * EX:(3 + b) * EX] for b in range(2)]
        F0 = 4 * EX
        cg, cv, sgt = RT[:, F0:F0 + T], RT[:, F0 + T:F0 + 2 * T], RT[:, F0 + 2 * T:F0 + 3 * T]
        for blk in range(4):
            for il in range(11):
                i = 11 * blk + il
                b = i % 2
                sgs, lgs = wslot(w["w_up"], 2 * i, KT)
                svs, lvs = wslot(w["w_up"], 2 * i + 1, KT)
                pg, pv = ps(), ps()
                mm(("ps", pg), PS[pg][:, :], [(lgs(k), hm(k)) for k in range(KT)], r=[("slot", sgs), "h"])
                mm(("ps", pv), PS[pv][:, :], [(lvs(k), hm(k)) for k in range(KT)], r=[("slot", svs), "h"])
                act(ug[b][:, 2:EX], PS[pg][:, :], AF.Copy, r=[("ps", pg)], w=[("ug", b)])
                P.op("pool", lambda e, i=i, b=b: e.tensor_copy(out=ug[b][:, 0:2], in_=ut[:, i, :]), r=[("ut", i), "UT"], w=[("ugh", b)])
                P.op("pool", lambda e, i=i, b=b: e.tensor_copy(out=ut[:, i, :], in_=ug[b][:, T:EX]), r=[("ug", b), ("ugh", b)], w=[("ut", i)])
                act(uv[b][:, 2:EX], PS[pv][:, :], AF.Copy, r=[("ps", pv)], w=[("uv", b)])
                P.op("pool", lambda e, i=i, b=b: e.tensor_copy(out=uv[b][:, 0:2], in_=ut[:, NHT + i, :]), r=[("ut", NHT + i), "UT"], w=[("uvh", b)])
                P.op("pool", lambda e, i=i, b=b: e.tensor_copy(out=ut[:, NHT + i, :], in_=uv[b][:, T:EX]), r=[("uv", b), ("uvh", b)], w=[("ut", NHT + i)])
                for (src, dst, key, tl) in ((ug[b], cg, "cg", i), (uv[b], cv, "cv", NHT + i)):
                    fw = lambda tap, tl=tl: VEC[:, l * V_N + V_FW + tl * 3 + tap:l * V_N + V_FW + tl * 3 + tap + 1]
                    sk = [("ug", b), ("ugh", b)] if key == "cg" else [("uv", b), ("uvh", b)]
                    P.op("dve", lambda e, src=src, dst=dst, fw=fw: e.tensor_scalar(out=dst, in0=src[:, 0:T], scalar1=fw(0), scalar2=None, op0=ALU.mult),
                         r=sk + ["VEC"], w=[key])
                    P.op("dve", lambda e, src=src, dst=dst, fw=fw: e.scalar_tensor_tensor(out=dst, in0=src[:, 1:T + 1], scalar=fw(1), in1=dst,
                                                                                         op0=ALU.mult, op1=ALU.add), r=sk + ["VEC", key], w=[key])
                    P.op("dve", lambda e, src=src, dst=dst, fw=fw: e.scalar_tensor_tensor(out=dst, in0=src[:, 2:T + 2], scalar=fw(2), in1=dst,
                                                                                         op0=ALU.mult, op1=ALU.add), r=sk + ["VEC", key], w=[key])
                act(sgt, cg, AF.Silu, r=["cg"], w=["sgt"])
                tt("pool", R(hid[:, il, :]), sgt, cv, ALU.mult, ["sgt", "cv"], [("hid", il)])
            hk = [("hid", k) for k in range(11)]
            for m in range(KT):
                si, sl = wslot(w["w_dn"], blk * 16 + m, 11)
                pd = ps()
                mm(("ps", pd), PS[pd][:, :], [(sl(k), R(hid[:, k, :])) for k in range(11)], r=[("slot", si)] + hk)
                tt("dve", xm(m), xm(m), PS[pd][:, :], ALU.add, [("ps", pd), "x"], ["x"])

    xTv = xT.rearrange("(k p) t -> p k t", p=128)
    oTv = outT.rearrange("(k p) t -> p k t", p=128)
    for q in range(NQ):
        P.barrier()
        P.dma("pool", [(xe[:, :, HAL:XW], xTv[:, :, q * T:(q + 1) * T])], "xin", w=["x"])
        for l in range(NL):
            chunk_layer(q, l)
        P.barrier()
        ot = v3(RT, 3 * XW, 4, T)
        rmsnorm(lambda k: xe[:, k, HAL:XW], None, 0, None, None, lambda k: ot[:, k % 4, :], None, ["x"], ["ot"],
                after=lambda k: (P.dma("pool", [(oTv[:, k - 3:k + 1, q * T:(q + 1) * T], ot)], "xout", r=["ot"], w=["outT"])
                                 if k % 4 == 3 else None))
    if dbg:
        dbg[0](P, locals())
    P.final_wait("pool")
    es.close()
    return nc


def _slots(Wm):
    K, M = Wm.shape
    kt, mt = K // 128, M // 128
    return np.ascontiguousarray(Wm.reshape(kt, 128, mt, 128).transpose(2, 1, 0, 3)).reshape(mt * 128, kt * 128)


def _tiles(Wm, order):
    return np.concatenate([Wm[:, t * 128:(t + 1) * 128] for t in order], axis=1)


def _col(v):
    n = v.shape[0] // 128
    return np.ascontiguousarray(v.reshape(n, 128).T)


def _prep_layer(inp, l):
    f = lambda a: np.asarray(a, dtype=np.float32)
    o = {}
    w_in = f(inp["w_in"][l])
    order = []
    for i in range(8):
        order += [i, 8 + i]
    order += list(range(16, 24))
    for m in range(16):
        order += [24 + m, 40 + m]
    o[f"w_in{l}"] = _slots(_tiles(w_in, order))
    o[f"w_pw{l}"] = _slots(f(inp["conv_w_pw"][l]))
    glu = f(inp["ssm_w_glu"][l])
    order = []
    for m in range(16):
        order += [m, 16 + m]
    o[f"w_glu{l}"] = _slots(_tiles(glu, order))
    o[f"w_out{l}"] = _slots(f(inp["w_out"][l]))
    o[f"w_q{l}"] = _slots(f(inp["xa_w_q"][l]))
    kv = f(inp["xa_w_kv"][l])
    o[f"w_k{l}"] = _slots(kv[:, :D])
    o[f"w_v{l}"] = _slots(kv[:, D:])
    o[f"w_o{l}"] = _slots(f(inp["xa_w_o"][l]))
    up = f(inp["ffn_w_up"][l])
    order = []
    for i in range(NHT):
        order += [i, NHT + i]
    o[f"w_up{l}"] = _slots(_tiles(up, order))
    dn = f(inp["ffn_w_down"][l])
    o[f"w_dn{l}"] = np.concatenate([_slots(dn[b * 1408:(b + 1) * 1408, :]) for b in range(4)], axis=0)
    a_re = f(inp["ssm_a_re"][l]).reshape(4096)
    a_im = f(inp["ssm_a_im"][l]).reshape(4096)
    ldt = np.repeat(f(inp["ssm_log_dt"][l]), 64)
    o[f"ssm_col{l}"] = np.concatenate([_col(a_re), _col(a_im), _col(ldt)], axis=1)
    o[f"ssm_row{l}"] = np.ascontiguousarray(np.broadcast_to(np.concatenate([a_re, a_im, ldt])[None, :], (128, 3 * 4096)))
    B_re = f(inp["ssm_b_re"][l])
    B_im = f(inp["ssm_b_im"][l])
    C_re = f(inp["ssm_c_re"][l])
    C_im = f(inp["ssm_c_im"][l])
    bt = np.zeros((2, 128, 32, 128), np.float32)
    ct = np.zeros((128, 32, 2, 128), np.float32)
    for j in range(32):
        i = j // 4
        for gg in range(2):
            g = 2 * j + gg
            k0 = 16 * (g - 8 * i)
            bt[0, k0:k0 + 16, j, gg * 64:(gg + 1) * 64] = B_re[g].T
            bt[1, k0:k0 + 16, j, gg * 64:(gg + 1) * 64] = B_im[g].T
            ct[gg * 64:(gg + 1) * 64, j, 0, k0:k0 + 16] = C_re[g].T
            ct[gg * 64:(gg + 1) * 64, j, 1, k0:k0 + 16] = C_im[g].T
    o[f"bt{l}"] = np.concatenate([bt[0].reshape(128, 4096), bt[1].reshape(128, 4096)], axis=1)
    o[f"ct{l}"] = ct.reshape(128, 32 * 256)
    vec = np.zeros((128, V_N), np.float32)
    vec[:, V_MIXG:V_MIXG + 16] = _col(f(inp["mix_norm_g"][l]))
    vec[:, V_XAG:V_XAG + 16] = _col(f(inp["xa_norm_g"][l]))
    vec[:, V_MEMG:V_MEMG + 16] = _col(f(inp["mem_norm_g"][l]))
    vec[:, V_FFNG:V_FFNG + 16] = _col(f(inp["ffn_norm_g"][l]))
    vec[:, V_CB:V_CB + 8] = _col(f(inp["conv_dw_b"][l]))
    vec[:, V_LNG:V_LNG + 8] = _col(f(inp["conv_ln_g"][l]))
    vec[:, V_LNB:V_LNB + 8] = _col(f(inp["conv_ln_b"][l]))
    vec[:, V_SD:V_SD + 8] = _col(f(inp["ssm_d"][l]))
    cw = f(inp["conv_dw_w"][l])
    vec[:, V_CW:V_CW + 248] = cw.T.reshape(8, 128, 31).transpose(1, 0, 2).reshape(128, 248)
    fw = f(inp["ffn_dw_w"][l])
    vec[:, V_FW:V_FW + 264] = fw.T.reshape(88, 128, 3).transpose(1, 0, 2).reshape(128, 264)
    return o, vec


_CACHE = {}


def _run(inputs, NQ=8, NL=2, dbg=None, cores=2):
    key = (NQ, NL, dbg is not None)
    if key not in _CACHE:
        _CACHE[key] = build_program(NQ, NL, dbg)
    nc = _CACHE[key]
    shared = {}
    vecs = np.zeros((128, NL * V_N + 16), np.float32)
    for l in range(NL):
        o, vec = _prep_layer(inputs, l)
        shared.update(o)
        vecs[:, l * V_N:(l + 1) * V_N] = vec
    vecs[:, NL * V_N:] = _col(np.asarray(inputs["final_norm_g"], np.float32))
    shared["vecs"] = vecs
    shared["ident"] = np.eye(128, dtype=np.float32)
    shared["ones"] = np.ones((128, 128), np.float32)
    x = np.asarray(inputs["x"], np.float32)
    mem = np.asarray(inputs["mem"], np.float32)
    in_maps = []
    for b in range(cores):
        m = dict(shared)
        m["xT"] = np.ascontiguousarray(x[b].T)
        m["memT"] = np.ascontiguousarray(mem[b].T)
        in_maps.append(m)
    res = run_bass_kernel_spmd(nc, in_maps, core_ids=list(range(cores)))
    return res


def kernel(**inputs):
    res = _run(inputs)
    out = np.stack([np.ascontiguousarray(res.results[b]["outT"].T) for b in range(2)], axis=0)
    return out.astype(np.float32)
```
